# Optimizing a Trainium2 kernel written in Bass

```python
import math
import jax
import jax.numpy as jnp
from jax import lax
import numpy as np

D_MODEL = 1024
BATCH = 8
SEQ = 2048
DEPTH = 4

CTX_LEN = 256
GRID_W = 64
N_EVEN = (DEPTH + 1) // 2
N_ODD = DEPTH // 2
CHUNK = 64
MLSTM_HEADS = 4
MLSTM_HD = 128
MLSTM_W = MLSTM_HEADS * MLSTM_HD
CONF_CH = D_MODEL - MLSTM_W
CONF_K = 31
GDN_HEADS = 8
GDN_HD = 128
GDN_W = GDN_HEADS * GDN_HD
SHORT_K = 5
FFN = 2816
FFN_K = 3
FORGET_BIAS = 3.0
EPS = 1e-6
EVEN_IN = 4 * MLSTM_W + 4 * MLSTM_HEADS + 2 * CONF_CH
ODD_IN = 4 * GDN_W + 4 * GDN_HEADS
F32 = jnp.float32

kernel_name = "hybrid_mlstm_conformer_gdn_dit"


def _rms(x):
    xf = x.astype(F32)
    return xf * lax.rsqrt(jnp.mean(xf * xf, axis=-1, keepdims=True) + EPS)


def rms_norm(x, g):
    return (_rms(x) * g.astype(F32)).astype(x.dtype)


def ada_rms_norm(x, g, shift, scale):
    y = _rms(x) * g.astype(F32) * (1.0 + scale.astype(F32)) + shift.astype(F32)
    return y.astype(x.dtype)


def layer_norm(x, g, b):
    xf = x.astype(F32)
    mu = jnp.mean(xf, axis=-1, keepdims=True)
    xc = xf - mu
    var = jnp.mean(xc * xc, axis=-1, keepdims=True)
    return (xc * lax.rsqrt(var + EPS) * g.astype(F32) + b.astype(F32)).astype(x.dtype)


def l2_normalize(x):
    return x * lax.rsqrt(jnp.sum(x * x, axis=-1, keepdims=True) + EPS)


def dwconv1d(x, w):
    pad = w.shape[0] // 2
    return lax.conv_general_dilated(x, w[:, None, :].astype(x.dtype), (1,), [(pad, pad)],
                                    dimension_numbers=("NWC", "WIO", "NWC"),
                                    feature_group_count=x.shape[-1])


def dwconv2d_grid(x, w):
    bsz, t, ch = x.shape
    rows = t // GRID_W
    xg = x.reshape(bsz, rows, GRID_W, ch)
    ph, pw = w.shape[0] // 2, w.shape[1] // 2
    y = lax.conv_general_dilated(xg, w[:, :, None, :].astype(x.dtype), (1, 1), [(ph, ph), (pw, pw)],
                                 dimension_numbers=("NHWC", "HWIO", "NHWC"),
                                 feature_group_count=ch)
    return y.reshape(bsz, t, ch)


def _heads(a, n_heads, hd):
    bsz, t = a.shape[:2]
    return a.reshape(bsz, t, n_heads, hd).transpose(0, 2, 1, 3).astype(F32)


def _chunks(a):
    return a.reshape(a.shape[:2] + (a.shape[2] // CHUNK, CHUNK) + a.shape[3:])


def _unchunk(a):
    return a.reshape(a.shape[:2] + (-1,) + a.shape[4:])


def mlstm_direction(args, state, need_out):
    q, k, v, li, lf = (_chunks(a) for a in args)
    b = jnp.cumsum(lf, axis=-1)
    b_last = b[..., -1]
    w_log = b_last[..., None] - b + li
    m_loc = jnp.max(w_log, axis=-1)
    wgt = jnp.exp(w_log - m_loc[..., None])
    c_loc = jnp.einsum("bhnl,bhnlk,bhnlv->bhnkv", wgt, k, v)
    n_loc = jnp.einsum("bhnl,bhnlk->bhnk", wgt, k)

    def step(carry, inp):
        c_st, n_st, m_st = carry
        c_l, n_l, m_l, b_l = inp
        m_new = jnp.maximum(b_l + m_st, m_l)
        a_old = jnp.exp(b_l + m_st - m_new)
        a_loc = jnp.exp(m_l - m_new)
        c_new = a_old[..., None, None] * c_st + a_loc[..., None, None] * c_l
        n_new = a_old[..., None] * n_st + a_loc[..., None] * n_l
        return (c_new, n_new, m_new), (c_st, n_st, m_st)

    xs = tuple(jnp.moveaxis(a, 2, 0) for a in (c_loc, n_loc, m_loc, b_last))
    final, starts = lax.scan(step, state, xs)
    if not need_out:
        return None, final
    c0, n0, m0 = (jnp.moveaxis(a, 0, 2) for a in starts)
    causal = jnp.tril(jnp.ones((CHUNK, CHUNK), bool))
    d_log = jnp.where(causal, b[..., :, None] - b[..., None, :] + li[..., None, :], -jnp.inf)
    inter_log = b + m0[..., None]
    m_t = jnp.maximum(inter_log, jnp.max(d_log, axis=-1))
    p = jnp.exp(d_log - m_t[..., None]) * jnp.einsum("bhntk,bhnsk->bhnts", q, k)
    a_inter = jnp.exp(inter_log - m_t)
    num = (a_inter[..., None] * jnp.einsum("bhntk,bhnkv->bhntv", q, c0)
           + jnp.einsum("bhnts,bhnsv->bhntv", p, v))
    den = a_inter * jnp.einsum("bhntk,bhnk->bhnt", q, n0) + jnp.sum(p, axis=-1)
    h = num / jnp.maximum(jnp.abs(den), jnp.exp(-m_t))[..., None]
    return _unchunk(h), final


def gdn_direction(args, state, need_out):
    q, k, v, g, beta = (_chunks(a) for a in args)
    dv = v.shape[-1]
    incl = jnp.tril(jnp.ones((CHUNK, CHUNK), bool))
    strict = jnp.tril(jnp.ones((CHUNK, CHUNK), bool), -1)
    gc = jnp.cumsum(g, axis=-1)
    decay = jnp.exp(jnp.where(incl, gc[..., :, None] - gc[..., None, :], -jnp.inf))
    a_mat = jnp.where(strict, beta[..., :, None] * jnp.einsum("bhntk,bhnsk->bhnts", k, k) * decay, 0.0)
    rhs = jnp.concatenate([beta[..., None] * v, (beta * jnp.exp(gc))[..., None] * k], axis=-1)
    sol = lax.linalg.triangular_solve(a_mat, rhs, left_side=True, lower=True, unit_diagonal=True)
    u, w = sol[..., :dv], sol[..., dv:]
    g_last = gc[..., -1]
    k_dec = jnp.exp(g_last[..., None] - gc)[..., None] * k

    def step(s, inp):
        u_c, w_c, kd_c, gl_c = inp
        v_new = u_c - jnp.einsum("bhlk,bhkv->bhlv", w_c, s)
        s_new = jnp.exp(gl_c)[..., None, None] * s + jnp.einsum("bhlk,bhlv->bhkv", kd_c, v_new)
        return s_new, (s, v_new)

    xs = tuple(jnp.moveaxis(a, 2, 0) for a in (u, w, k_dec, g_last))
    final, (s0, v_new) = lax.scan(step, state, xs)
    if not need_out:
        return None, final
    s0 = jnp.moveaxis(s0, 0, 2)
    v_new = jnp.moveaxis(v_new, 0, 2)
    qk = jnp.einsum("bhntk,bhnsk->bhnts", q, k) * decay
    o = (jnp.einsum("bhntk,bhnkv->bhntv", jnp.exp(gc)[..., None] * q, s0)
         + jnp.einsum("bhnts,bhnsv->bhntv", qk, v_new))
    return _unchunk(o), final


def _flip(args):
    return tuple(jnp.flip(a, axis=2) for a in args)


def bidirectional(dir_fn, ctx_f, ctx_b, lat_f, lat_b, init, need_ctx):
    yc_f, sc_f = dir_fn(ctx_f, init, need_ctx)
    yc_b, sc_b = dir_fn(_flip(ctx_b), init, need_ctx)
    yl_f, _ = dir_fn(lat_f, sc_f, True)
    yl_b, _ = dir_fn(_flip(lat_b), sc_b, True)
    y_lat = yl_f + jnp.flip(yl_b, axis=2)
    y_ctx = yc_f + jnp.flip(yc_b, axis=2) if need_ctx else None
    return y_ctx, y_lat


def conformer_conv(glu, dw, dw_b, ln_g, ln_b):
    a, gt = jnp.split(glu, 2, axis=-1)
    y = dwconv1d(a * jax.nn.sigmoid(gt), dw) + dw_b
    return jax.nn.silu(layer_norm(y, ln_g, ln_b))


def even_mixer(hc, hl, w_in, b_in, head_g, conf_dw, conf_dw_b, conf_ln_g, conf_ln_b, w_out, need_ctx):
    W, H = MLSTM_W, MLSTM_HEADS

    def project(h):
        p = h @ w_in + b_in
        bsz, t = p.shape[:2]
        q = _heads(p[..., 0:W], H, MLSTM_HD) * (MLSTM_HD ** -0.5)
        k = _heads(p[..., W:2 * W], H, MLSTM_HD)
        v = _heads(p[..., 2 * W:3 * W], H, MLSTM_HD)
        og = p[..., 3 * W:4 * W]
        gates = p[..., 4 * W:4 * W + 4 * H].astype(F32).reshape(bsz, t, 4, H).transpose(2, 0, 3, 1)
        glu = p[..., 4 * W + 4 * H:]
        fwd = (q, k, v, gates[0], jax.nn.log_sigmoid(gates[1]))
        bwd = (q, k, v, gates[2], jax.nn.log_sigmoid(gates[3]))
        return fwd, bwd, og, glu

    def finish(y, og, glu):
        yh = _rms(jnp.swapaxes(y, 1, 2))
        bsz, t = yh.shape[:2]
        m_out = yh.reshape(bsz, t, W) * head_g.astype(F32) * jax.nn.sigmoid(og.astype(F32))
        c_out = conformer_conv(glu, conf_dw, conf_dw_b, conf_ln_g, conf_ln_b)
        return jnp.concatenate([m_out.astype(og.dtype), c_out], axis=-1) @ w_out

    cf, cb, c_og, c_glu = project(hc)
    lf_, lb, l_og, l_glu = project(hl)
    bsz = hl.shape[0]
    init = (jnp.zeros((bsz, H, MLSTM_HD, MLSTM_HD), F32), jnp.zeros((bsz, H, MLSTM_HD), F32),
            jnp.zeros((bsz, H), F32))
    yc, yl = bidirectional(mlstm_direction, cf, cb, lf_, lb, init, need_ctx)
    out_l = finish(yl, l_og, l_glu)
    out_c = finish(yc, c_og, c_glu) if need_ctx else None
    return out_c, out_l


def odd_mixer(hc, hl, w_in, short_w, a_log, dt_bias, head_g, w_out, need_ctx):
    W, H = GDN_W, GDN_HEADS
    a_rate = jnp.exp(a_log.astype(F32))
    dtb = dt_bias.astype(F32)

    def project(h):
        p = h @ w_in
        bsz, t = p.shape[:2]
        qkv = jax.nn.silu(dwconv1d(p[..., :3 * W], short_w))
        q = l2_normalize(_heads(qkv[..., 0:W], H, GDN_HD)) * (GDN_HD ** -0.5)
        k = l2_normalize(_heads(qkv[..., W:2 * W], H, GDN_HD))
        v = _heads(qkv[..., 2 * W:3 * W], H, GDN_HD)
        z = p[..., 3 * W:4 * W]
        ab = p[..., 4 * W:].astype(F32).reshape(bsz, t, 4, H).transpose(2, 0, 3, 1)
        g_f = -a_rate[0][None, :, None] * jax.nn.softplus(ab[0] + dtb[0][None, :, None])
        g_b = -a_rate[1][None, :, None] * jax.nn.softplus(ab[2] + dtb[1][None, :, None])
        fwd = (q, k, v, g_f, jax.nn.sigmoid(ab[1]))
        bwd = (q, k, v, g_b, jax.nn.sigmoid(ab[3]))
        return fwd, bwd, z

    def finish(y, z):
        yh = _rms(jnp.swapaxes(y, 1, 2)) * head_g.astype(F32)
        bsz, t = yh.shape[:2]
        zz = jax.nn.silu(z.astype(F32)).reshape(bsz, t, H, GDN_HD)
        return (yh * zz).reshape(bsz, t, W).astype(z.dtype) @ w_out

    cf, cb, c_z = project(hc)
    lf_, lb, l_z = project(hl)
    init = jnp.zeros((hl.shape[0], H, GDN_HD, GDN_HD), F32)
    yc, yl = bidirectional(gdn_direction, cf, cb, lf_, lb, init, need_ctx)
    out_l = finish(yl, l_z)
    out_c = finish(yc, c_z) if need_ctx else None
    return out_c, out_l


def conv_ffn(h, w_up, dw, dw_b, w_down, on_grid):
    gate, val = jnp.split(h @ w_up, 2, axis=-1)
    gate = dwconv2d_grid(gate, dw) if on_grid else dwconv1d(gate, dw[FFN_K // 2])
    return (jax.nn.silu(gate + dw_b) * val) @ w_down


def setup_inputs(seed: int = 0) -> dict:
    key = jax.random.key(seed)
    ks = iter(jax.random.split(key, 40))
    D = D_MODEL

    def nrm(shape, s):
        return jax.random.normal(next(ks), shape, F32) * s

    x = nrm((BATCH, SEQ, D), 1.0)
    c = nrm((BATCH, D), 1.0)
    ctx = nrm((BATCH, CTX_LEN, D), 1.0)
    c_ctx = nrm((D,), 1.0)
    ada_w = nrm((DEPTH, D, 6 * D), 0.5 * D ** -0.5)
    ada_b = nrm((DEPTH, 6 * D), 0.02)
    norm1_g = 1.0 + nrm((DEPTH, D), 0.02)
    norm2_g = 1.0 + nrm((DEPTH, D), 0.02)
    e_w_in = nrm((N_EVEN, D, EVEN_IN), D ** -0.5)
    f0 = 4 * MLSTM_W + MLSTM_HEADS
    f1 = 4 * MLSTM_W + 3 * MLSTM_HEADS
    e_b_in = nrm((N_EVEN, EVEN_IN), 0.02)
    e_b_in = e_b_in.at[:, f0:f0 + MLSTM_HEADS].add(FORGET_BIAS).at[:, f1:f1 + MLSTM_HEADS].add(FORGET_BIAS)
    e_head_g = 1.0 + nrm((N_EVEN, MLSTM_W), 0.02)
    e_conf_dw = nrm((N_EVEN, CONF_K, CONF_CH), CONF_K ** -0.5)
    e_conf_dw_b = nrm((N_EVEN, CONF_CH), 0.02)
    e_conf_ln_g = 1.0 + nrm((N_EVEN, CONF_CH), 0.02)
    e_conf_ln_b = nrm((N_EVEN, CONF_CH), 0.02)
    e_w_out = nrm((N_EVEN, D, D), D ** -0.5)
    o_w_in = nrm((N_ODD, D, ODD_IN), D ** -0.5)
    o_short_w = nrm((N_ODD, SHORT_K, 3 * GDN_W), SHORT_K ** -0.5)
    o_a_log = jnp.log(jax.random.uniform(next(ks), (N_ODD, 2, GDN_HEADS), F32, 1.0, 16.0))
    dt = jnp.exp(jax.random.uniform(next(ks), (N_ODD, 2, GDN_HEADS), F32, math.log(1e-3), math.log(1e-1)))
    o_dt_bias = dt + jnp.log(-jnp.expm1(-dt))
    o_head_g = 1.0 + nrm((N_ODD, GDN_HD), 0.02)
    o_w_out = nrm((N_ODD, GDN_W, D), GDN_W ** -0.5)
    f_w_up = nrm((DEPTH, D, 2 * FFN), D ** -0.5)
    f_dw = nrm((DEPTH, FFN_K, FFN_K, FFN), 1.0 / FFN_K)
    f_dw_b = nrm((DEPTH, FFN), 0.02)
    f_w_down = nrm((DEPTH, FFN, D), FFN ** -0.5)
    final_g = 1.0 + nrm((D,), 0.02)
    return {"x": x, "c": c, "ctx": ctx, "c_ctx": c_ctx, "ada_w": ada_w, "ada_b": ada_b,
            "norm1_g": norm1_g, "norm2_g": norm2_g,
            "e_w_in": e_w_in, "e_b_in": e_b_in, "e_head_g": e_head_g, "e_conf_dw": e_conf_dw,
            "e_conf_dw_b": e_conf_dw_b, "e_conf_ln_g": e_conf_ln_g, "e_conf_ln_b": e_conf_ln_b,
            "e_w_out": e_w_out, "o_w_in": o_w_in, "o_short_w": o_short_w, "o_a_log": o_a_log,
            "o_dt_bias": o_dt_bias, "o_head_g": o_head_g, "o_w_out": o_w_out,
            "f_w_up": f_w_up, "f_dw": f_dw, "f_dw_b": f_dw_b, "f_w_down": f_w_down,
            "final_g": final_g}


def reference(x, c, ctx, c_ctx, ada_w, ada_b, norm1_g, norm2_g,
              e_w_in, e_b_in, e_head_g, e_conf_dw, e_conf_dw_b, e_conf_ln_g, e_conf_ln_b, e_w_out,
              o_w_in, o_short_w, o_a_log, o_dt_bias, o_head_g, o_w_out,
              f_w_up, f_dw, f_dw_b, f_w_down, final_g):
    silu_c = jax.nn.silu(c)
    silu_cc = jax.nn.silu(c_ctx)
    for layer in range(DEPTH):
        need_ctx = layer < DEPTH - 1
        mod_l = (silu_c @ ada_w[layer] + ada_b[layer])[:, None, :]
        mod_c = (silu_cc @ ada_w[layer] + ada_b[layer])[None, None, :]
        sh1_l, sc1_l, g1_l, sh2_l, sc2_l, g2_l = jnp.split(mod_l, 6, axis=-1)
        sh1_c, sc1_c, g1_c, sh2_c, sc2_c, g2_c = jnp.split(mod_c, 6, axis=-1)
        hl = ada_rms_norm(x, norm1_g[layer], sh1_l, sc1_l)
        hc = ada_rms_norm(ctx, norm1_g[layer], sh1_c, sc1_c)
        j = layer // 2
        if layer % 2 == 0:
            yc, yl = even_mixer(hc, hl, e_w_in[j], e_b_in[j], e_head_g[j], e_conf_dw[j], e_conf_dw_b[j],
                                e_conf_ln_g[j], e_conf_ln_b[j], e_w_out[j], need_ctx)
        else:
            yc, yl = odd_mixer(hc, hl, o_w_in[j], o_short_w[j], o_a_log[j], o_dt_bias[j], o_head_g[j],
                               o_w_out[j], need_ctx)
        x = x + g1_l * yl
        hl = ada_rms_norm(x, norm2_g[layer], sh2_l, sc2_l)
        x = x + g2_l * conv_ffn(hl, f_w_up[layer], f_dw[layer], f_dw_b[layer], f_w_down[layer], True)
        if need_ctx:
            ctx = ctx + g1_c * yc
            hc = ada_rms_norm(ctx, norm2_g[layer], sh2_c, sc2_c)
            ctx = ctx + g2_c * conv_ffn(hc, f_w_up[layer], f_dw[layer], f_dw_b[layer], f_w_down[layer], False)
    return rms_norm(x, final_g)
```

```python
import numpy as np
from contextlib import ExitStack
import concourse.bass as bass
import concourse.mybir as mybir
from concourse.bass_utils import run_bass_kernel_spmd

F32 = mybir.dt.float32
BF16 = mybir.dt.bfloat16
AF = mybir.ActivationFunctionType
ALU = mybir.AluOpType
AX = mybir.AxisListType

NT = 2304
NTILE = 18
D = 1024
KD = 8
NCTX = 256
TB = [(0, 256), (256, 512), (768, 512), (1280, 512), (1792, 512)]
EPS = 1e-6
FFN = 2816
NJ = 22
BIG = 30000.0
DEPTH = 4


class Buf:
    __slots__ = ("w", "r", "prev")

    def __init__(self):
        self.w = {}
        self.r = {}
        self.prev = {}


class View:
    __slots__ = ("ap", "bufs", "psum")

    def __init__(self, ap, bufs, psum=False):
        self.ap = ap
        self.bufs = bufs
        self.psum = psum


class Tile:
    def __init__(self, t, psum=False):
        self.t = t
        self.buf = Buf()
        self.psum = psum

    def __getitem__(self, idx):
        return View(self.t[idx], [self.buf], self.psum)

    def v(self, ap):
        return View(ap, [self.buf], self.psum)


class Eng:
    def __init__(self, name, obj):
        self.name = name
        self.obj = obj
        self.cnt = 0
        self.csems = []
        self.seen = {}
        self.dma_n = 0
        self.rings = []


CEPOCH = 30000
RK = 8
REPOCH = 1800


class Prog:
    def __init__(self, nc, es):
        self.nc = nc
        self.es = es
        self.sems = []
        self.eng = {
            "pe": Eng("pe", nc.tensor),
            "act": Eng("act", nc.scalar),
            "dve": Eng("dve", nc.vector),
            "pool": Eng("pool", nc.gpsimd),
            "sp": Eng("sp", nc.sync),
        }
        self.ninst = 0
        self.nwait = 0

    def new_sem(self):
        s = self.es.enter_context(self.nc.semaphore("s%d" % len(self.sems)))
        self.sems.append(s)
        return len(self.sems) - 1

    def _ctag(self, e):
        ep = e.cnt // CEPOCH
        while len(e.csems) <= ep:
            e.csems.append(self.new_sem())
        e.cnt += 1
        return e.csems[ep], (e.cnt - 1) % CEPOCH + 1

    def _dtag(self, e, n):
        ep = n // (RK * REPOCH)
        while len(e.rings) <= ep:
            e.rings.append([self.new_sem() for _ in range(RK)])
        slot = n % RK
        return e.rings[ep][slot], 16 * ((n // RK) % REPOCH + 1)

    def op(self, en, fn, outs, ins, partial=False, dma=False):
        e = self.eng[en]
        deps = {}

        def add(tagd, skip_same):
            for sem, (val, ten, kind) in tagd.items():
                if skip_same and kind == "c" and ten == en:
                    continue
                if deps.get(sem, 0) < val:
                    deps[sem] = val

        for v in ins:
            for b in v.bufs:
                add(b.w, en == "pe")
                if v.psum:
                    add(b.r, True)
        for v in outs:
            for b in v.bufs:
                if partial and not b.r:
                    add(b.prev, True)
                    continue
                add(b.w, True)
                add(b.r, True)
        if dma and e.dma_n >= RK:
            ps, pv = self._dtag(e, e.dma_n - RK)
            if deps.get(ps, 0) < pv:
                deps[ps] = pv
        for sem, val in deps.items():
            if e.seen.get(sem, 0) >= val:
                continue
            e.obj.wait_ge(self.sems[sem], val)
            e.seen[sem] = val
            self.nwait += 1
        inst = fn()
        if dma:
            sem, val = self._dtag(e, e.dma_n)
            e.dma_n += 1
            inst.then_inc(self.sems[sem], 16)
            tag = (val, en, "d")
        else:
            sem, val = self._ctag(e)
            inst.then_inc(self.sems[sem], 1)
            tag = (val, en, "c")
        self.ninst += 1
        for v in ins:
            for b in v.bufs:
                if b.r.get(sem, (0,))[0] < val:
                    b.r[sem] = tag
        for v in outs:
            for b in v.bufs:
                if partial and not b.r:
                    b.w[sem] = tag
                else:
                    pv = dict(b.w)
                    for k_, t_ in b.r.items():
                        if pv.get(k_, (0,))[0] < t_[0]:
                            pv[k_] = t_
                    b.prev = pv
                    b.w = {sem: tag}
                    b.r = {}

    def barrier(self):
        tags = {}
        for e in self.eng.values():
            if e.cnt > 0:
                ep = (e.cnt - 1) // CEPOCH
                tags[e.csems[ep]] = (e.cnt - 1) % CEPOCH + 1
            for n in range(max(0, e.dma_n - RK), e.dma_n):
                s, v = self._dtag(e, n)
                if tags.get(s, 0) < v:
                    tags[s] = v
        for e in self.eng.values():
            for s, v in tags.items():
                if e.seen.get(s, 0) >= v:
                    continue
                e.obj.wait_ge(self.sems[s], v)
                e.seen[s] = v
                self.nwait += 1

    def mm(self, out, lhsT, rhs, start, stop):
        self.op("pe", lambda: self.nc.tensor.matmul(out.ap, lhsT.ap, rhs.ap, start=start, stop=stop),
                [out], [lhsT, rhs], partial=not start)

    def mm_part(self, out, lhsT, rhs, start, stop, partial):
        self.op("pe", lambda: self.nc.tensor.matmul(out.ap, lhsT.ap, rhs.ap, start=start, stop=stop),
                [out], [lhsT, rhs], partial=partial)

    def tr(self, out, in_, ident, partial=False):
        self.op("pe", lambda: self.nc.tensor.transpose(out=out.ap, in_=in_.ap, identity=ident.ap),
                [out], [in_, ident], partial=partial)

    def act(self, out, in_, func, bias=None, scale=None, partial=False, accum=None):
        ins = [in_]
        kw = {}
        if bias is not None:
            if isinstance(bias, View):
                ins.append(bias)
                kw["bias"] = bias.ap
            else:
                kw["bias"] = bias
        if scale is not None:
            if isinstance(scale, View):
                ins.append(scale)
                kw["scale"] = scale.ap
            else:
                kw["scale"] = scale
        outs = [out]
        if accum is not None:
            kw["accum_out"] = accum.ap
            outs.append(accum)
        self.op("act", lambda: self.nc.scalar.activation(out=out.ap, in_=in_.ap, func=func, **kw),
                outs, ins, partial=partial)

    def _veng(self, en):
        return self.nc.vector if en == "dve" else self.nc.gpsimd

    def tt(self, en, out, in0, in1, op, partial=False):
        o = self._veng(en)
        self.op(en, lambda: o.tensor_tensor(out=out.ap, in0=in0.ap, in1=in1.ap, op=op),
                [out], [in0, in1], partial=partial)

    def ts(self, en, out, in0, s1, s2, op0, op1=None, partial=False):
        o = self._veng(en)
        ins = [in0]
        a1 = s1
        a2 = s2
        if isinstance(s1, View):
            ins.append(s1)
            a1 = s1.ap
        if isinstance(s2, View):
            ins.append(s2)
            a2 = s2.ap
        if op1 is None:
            fn = lambda: o.tensor_scalar(out=out.ap, in0=in0.ap, scalar1=a1, scalar2=None, op0=op0)
        else:
            fn = lambda: o.tensor_scalar(out=out.ap, in0=in0.ap, scalar1=a1, scalar2=a2, op0=op0, op1=op1)
        self.op(en, fn, [out], ins, partial=partial)

    def stt(self, en, out, in0, scalar, in1, op0, op1, partial=False):
        en = "dve"
        o = self._veng(en)
        ins = [in0, in1]
        a = scalar
        if isinstance(scalar, View):
            ins.append(scalar)
            a = scalar.ap
        self.op(en, lambda: o.scalar_tensor_tensor(out=out.ap, in0=in0.ap, scalar=a, in1=in1.ap, op0=op0, op1=op1),
                [out], ins, partial=partial)

    def copy(self, en, out, in_, partial=False):
        if en == "act":
            self.act(out, in_, AF.Copy, partial=partial)
        else:
            o = self._veng(en)
            self.op(en, lambda: o.tensor_copy(out=out.ap, in_=in_.ap), [out], [in_], partial=partial)

    def memset(self, en, out, val):
        o = self._veng(en)
        self.op(en, lambda: o.memset(out.ap, val), [out], [])

    def recip(self, out, in_, partial=False):
        self.op("dve", lambda: self.nc.vector.reciprocal(out=out.ap, in_=in_.ap), [out], [in_], partial=partial)

    def dma(self, q, out, in_, partial=False, slow=False):
        o = self.nc.sync if q == "sp" else self.nc.gpsimd
        if slow:
            fn = lambda: o.dma_start(out=out.ap, in_=in_.ap, allow_slow_non_contiguous=True)
        else:
            fn = lambda: o.dma_start(out=out.ap, in_=in_.ap)
        self.op(q, fn, [out], [in_], partial=partial, dma=True)


def build_program(cfg):
    nc = bass.Bass("TRN2", target_bir_lowering=False)
    es = ExitStack()
    P = Prog(nc, es)
    dbg = cfg.get("debug", False)
    nlayers = cfg.get("layers", DEPTH)
    mixer_on = cfg.get("mixer", True)
    ffn_on = cfg.get("ffn", True)

    def din(name, shape):
        return nc.dram_tensor(name, list(shape), F32, kind="ExternalInput").ap()

    dbg_names = []

    def dscr(name, shape, dump=False, dt=F32):
        kind = "Internal"
        if dbg and dump:
            kind = "ExternalOutput"
            dbg_names.append(name)
        return nc.dram_tensor(name, list(shape), dt, kind=kind).ap()

    def sb(name, shape, dt=F32):
        return Tile(es.enter_context(nc.sbuf_tensor(name, list(shape), dt)))

    ucnt = [0]

    def ptile(ph, name, shape, dt=F32):
        ucnt[0] += 1
        return Tile(ph.enter_context(nc.sbuf_tensor("%s_u%d" % (name, ucnt[0]), list(shape), dt)))

    def load_w(dst, src_ap, stage, n, en="act", q="sp"):
        P.dma(q, stage[:, 0:n], View(src_ap, []))
        P.copy(en, dst[:, 0:n], stage[:, 0:n])

    xc_d = din("xc", [NT, D])
    cv_d = din("cvec", [128, KD, 2])
    adaw_d = din("ada_w", [DEPTH, 48, 128, KD, 128])
    adab_d = din("ada_b", [DEPTH, 128, 48])
    n1g_d = din("norm1_g", [DEPTH, 128, KD])
    n2g_d = din("norm2_g", [DEPTH, 128, KD])
    fing_d = din("final_g", [128, KD])
    if ffn_on:
        fwup_d = din("f_w_up", [DEPTH, 2 * NJ, 128, KD, 128])
        fdw_d = din("f_dw", [DEPTH, 128, NJ, 9])
        fdwb_d = din("f_dw_b", [DEPTH, 128, NJ])
        fwdn_d = din("f_w_down", [DEPTH, KD, 128, NJ, 128])
    out_d = nc.dram_tensor("out", [2048, D], F32, kind="ExternalOutput").ap()
    act_scr = dscr("act_scr", [NJ, 128, NT], dt=BF16) if ffn_on else None
    act_scr_b = [[Buf() for _ in TB] for _ in range(NJ)]

    xT = sb("xT", [128, KD * NT])
    aT = sb("aT", [128, KD * NT], BF16)
    xT3 = xT.t[:].rearrange("p (k t) -> p k t", k=KD)
    aT3 = aT.t[:].rearrange("p (k t) -> p k t", k=KD)
    ident = sb("ident", [128, 128])
    onesD = sb("onesD", [128, 128])
    modT = sb("modT", [128, 96])
    PS = [Tile(es.enter_context(nc.psum_tensor("ps%d" % i, [128, 512], F32)), psum=True) for i in range(8)]

    xT_b = [Buf() for _ in range(KD)]
    aT_b = [Buf() for _ in range(KD)]

    def xTv(kc, t0, n):
        return View(xT3[:, kc, t0:t0 + n], [xT_b[kc]])

    def aTv(kc, t0, n):
        return View(aT3[:, kc, t0:t0 + n], [aT_b[kc]])

    P.memset("pool", ident[:], 1.0)
    P.op("pool", lambda: nc.gpsimd.affine_select(out=ident.t[:], in_=ident.t[:], pattern=[[-1, 128]],
                                                  compare_op=ALU.is_equal, fill=0.0, base=0, channel_multiplier=1),
         [ident[:]], [ident[:]])
    P.memset("pool", onesD[:], 1.0 / D)

    with ExitStack() as ph:
        stg = [ptile(ph, "ldx%d" % i, [128, D]) for i in range(2)]
        for ti in range(NTILE):
            s = stg[ti % 2]
            P.dma("sp", s[:], View(xc_d[ti * 128:(ti + 1) * 128, :], []))
            for half in range(2):
                ps = PS[(ti * 2 + half) % 4]
                for q in range(4):
                    kc = half * 4 + q
                    P.tr(ps[:, q * 128:(q + 1) * 128], s[:, kc * 128:(kc + 1) * 128], ident[:], partial=(q > 0))
                dst = View(xT3[:, half * 4:(half + 1) * 4, ti * 128:(ti + 1) * 128], xT_b[half * 4:(half + 1) * 4])
                src = ps.v(ps.t[:].rearrange("p (q t) -> p q t", q=4))
                P.copy("act" if half == 0 else "dve", dst, src, partial=True)
        P.barrier()

    scv = sb("scv", [128, KD * 2])
    P.dma("sp", scv[:], View(cv_d.rearrange("p k j -> p (k j)"), []))
    P.act(scv[:], scv[:], AF.Silu)
    scv3 = scv.t[:].rearrange("p (k j) -> p k j", j=2)

    coef = sb("coef", [128, 6 * KD * 2])
    coef4 = coef.t[:].rearrange("p (g k j) -> p g k j", g=6, k=KD)
    nrm_g = sb("nrm_g", [128, 2 * KD])
    adab = sb("adab", [128, 48])
    ones_c = sb("ones_c", [128, 2 * KD])
    P.memset("pool", ones_c[:], 1.0)

    def modulation(l):
        with ExitStack() as ph:
            wb = [ptile(ph, "adw%d" % i, [128, KD * 128]) for i in range(3)]
            P.dma("pool", adab[:], View(adab_d[l], []))
            P.dma("pool", nrm_g[:, 0:KD], View(n1g_d[l], []))
            P.dma("pool", nrm_g[:, KD:2 * KD], View(n2g_d[l], []))
            ps = PS[7]
            for j in range(48):
                w = wb[j % 3]
                P.dma("sp" if j % 2 == 0 else "pool", w[:], View(adaw_d[l, j].rearrange("p k f -> p (k f)"), []))
                for kc in range(KD):
                    P.mm_part(ps[:, 2 * j:2 * j + 2], w[:, kc * 128:(kc + 1) * 128], scv.v(scv3[:, kc, :]),
                              start=(kc == 0), stop=(kc == KD - 1), partial=not (j == 0 and kc == 0))
            m3 = modT.t[:].rearrange("p (j k) -> p j k", k=2)
            P.tt("dve", modT.v(m3), ps.v(ps.t[:, 0:96].rearrange("p (j k) -> p j k", k=2)),
                 adab.v(adab.t[:].unsqueeze(2).to_broadcast([128, 48, 2])), ALU.add)
            def grp(g):
                return modT.v(m3[:, g * 8:(g + 1) * 8, :])
            for half, (gsc, gsh, gg) in enumerate([(1, 0, 2), (4, 3, 5)]):
                gam = nrm_g.v(nrm_g.t[:, half * KD:(half + 1) * KD].unsqueeze(2).to_broadcast([128, KD, 2]))
                A = coef.v(coef4[:, half * 3 + 0])
                P.stt("dve", A, grp(gsc), 1.0, gam, ALU.add, ALU.mult, partial=True)
                P.copy("dve", coef.v(coef4[:, half * 3 + 1]), grp(gsh), partial=True)
                P.copy("dve", coef.v(coef4[:, half * 3 + 2]), grp(gg), partial=True)
            P.barrier()

    def cf(g, kc, kind):
        return coef.v(coef4[:, g, kc, kind:kind + 1])

    def norm_phase(Afn, Bfn, tbs):
        with ExitStack() as ph:
            sq = [ptile(ph, "sq%d" % i, [128, 512]) for i in range(3)]
            rs = [ptile(ph, "rs%d" % i, [128, 512]) for i in range(2)]
            tm = [ptile(ph, "tm%d" % i, [128, 512]) for i in range(3)]
            n = 0
            for bi, (t0, tn) in enumerate(tbs):
                kind = 1 if t0 < NCTX else 0
                ps = PS[4 + bi % 2]
                for kc in range(KD):
                    s = sq[n % 3]
                    n += 1
                    P.act(s[:, 0:tn], xTv(kc, t0, tn), AF.Square)
                    P.mm(ps[:, 0:tn], onesD[:], s[:, 0:tn], start=(kc == 0), stop=(kc == KD - 1))
                r = rs[bi % 2]
                P.act(r[:, 0:tn], ps[:, 0:tn], AF.Sqrt, bias=EPS, scale=1.0)
                P.recip(r[:, 0:tn], r[:, 0:tn])
                for kc in range(KD):
                    t = tm[kc % 3]
                    P.tt("dve", t[:, 0:tn], xTv(kc, t0, tn), r[:, 0:tn], ALU.mult)
                    P.act(aTv(kc, t0, tn), t[:, 0:tn], AF.Identity, bias=Bfn(kc, kind), scale=Afn(kc, kind),
                          partial=True)
            P.barrier()

    fdw = sb("fdw", [128, NJ * 9])
    fdwb = sb("fdwb", [128, NJ])
    fdw3 = fdw.t[:].rearrange("p (j k) -> p j k", k=9)

    def ffn_phase(l):
        P.dma("pool", fdw[:], View(fdw_d[l].rearrange("p j k -> p (j k)"), []))
        P.dma("pool", fdwb[:], View(fdwb_d[l], []))
        with ExitStack() as ph:
            wgs = [ptile(ph, "wgs%d" % i, [128, KD * 128]) for i in range(2)]
            wvs = [ptile(ph, "wvs%d" % i, [128, KD * 128]) for i in range(2)]
            wg = [ptile(ph, "wg%d" % i, [128, KD * 128], BF16) for i in range(2)]
            wv = [ptile(ph, "wv%d" % i, [128, KD * 128], BF16) for i in range(2)]
            G = [ptile(ph, "G%d" % i, [128, NT]) for i in range(1)]
            Gc = [ptile(ph, "Gc%d" % i, [128, NT]) for i in range(2)]
            gcp = [ptile(ph, "gcp%d" % i, [128, NT - NCTX]) for i in range(1)]
            ptmp = ptile(ph, "ptmp", [128, NT - NCTX])
            Vv = [ptile(ph, "Vv%d" % i, [128, NT]) for i in range(2)]
            gh = [ptile(ph, "gh%d" % i, [128, NT], BF16) for i in range(1)]
            npb = 0
            for j in range(NJ):
                g_w = wg[j % 2]
                v_w = wv[j % 2]
                load_w(g_w, fwup_d[l, j].rearrange("p k f -> p (k f)"), wgs[j % 2], KD * 128)
                load_w(v_w, fwup_d[l, NJ + j].rearrange("p k f -> p (k f)"), wvs[j % 2], KD * 128, q="pool")
                g = G[0]
                gc = Gc[j % 2]
                vv = Vv[j % 2]
                for bi, (t0, tn) in enumerate(TB):
                    for which, (w, dst, en) in enumerate([(g_w, g, "act"), (v_w, vv, "act")]):
                        ps = PS[npb % 4]
                        npb += 1
                        for kc in range(KD):
                            P.mm(ps[:, 0:tn], w[:, kc * 128:(kc + 1) * 128], aTv(kc, t0, tn),
                                 start=(kc == 0), stop=(kc == KD - 1))
                        P.copy(en, dst[:, t0:t0 + tn], ps[:, 0:tn], partial=True)
                def wk(i, jj):
                    return fdw.v(fdw3[:, j, i * 3 + jj:i * 3 + jj + 1])
                P.ts("dve", gc[:, 0:NCTX], g[:, 0:NCTX], wk(1, 1), None, ALU.mult, partial=True)
                P.stt("pool", gc[:, 1:NCTX], g[:, 0:NCTX - 1], wk(1, 0), gc[:, 1:NCTX], ALU.mult, ALU.add, partial=True)
                P.stt("pool", gc[:, 0:NCTX - 1], g[:, 1:NCTX], wk(1, 2), gc[:, 0:NCTX - 1], ALU.mult, ALU.add, partial=True)
                g3 = g.t[:, NCTX:NT].rearrange("p (r w) -> p r w", w=64)
                c3 = gc.t[:, NCTX:NT].rearrange("p (r w) -> p r w", w=64)
                P.ts("dve", gc[:, NCTX:NT], g[:, NCTX:NT], wk(1, 1), None, ALU.mult, partial=True)
                k = 0
                gp = gcp[0]
                p3 = gp.t[:].rearrange("p (r w) -> p r w", w=64)
                t3 = ptmp.t[:].rearrange("p (r w) -> p r w", w=64)
                for i in range(3):
                    for jj in range(3):
                        if i == 1 and jj == 1:
                            continue
                        dr, dc = i - 1, jj - 1
                        r0, r1 = max(0, -dr), min(32, 32 - dr)
                        c0, c1 = max(0, -dc), min(64, 64 - dc)
                        src = g.v(g3[:, r0 + dr:r1 + dr, c0 + dc:c1 + dc])
                        if False:
                            P.ts("pool", ptmp.v(t3[:, r0:r1, c0:c1]), src, wk(i, jj), None, ALU.mult, partial=True)
                            P.tt("pool", gp.v(p3[:, r0:r1, c0:c1]), gp.v(p3[:, r0:r1, c0:c1]),
                                 ptmp.v(t3[:, r0:r1, c0:c1]), ALU.add, partial=True)
                        else:
                            P.stt("dve", gc.v(c3[:, r0:r1, c0:c1]), src,
                                  wk(i, jj), gc.v(c3[:, r0:r1, c0:c1]), ALU.mult, ALU.add, partial=True)
                        k += 1
                P.act(gc[:], gc[:], AF.Silu, bias=fdwb[:, j:j + 1], scale=1.0)
                P.tt("dve", gh[0][:], gc[:], vv[:], ALU.mult)
                for bi, (t0, tn) in enumerate(TB):
                    P.dma("pool", View(act_scr[j, :, t0:t0 + tn], [act_scr_b[j][bi]]), gh[0][:, t0:t0 + tn])
            P.barrier()
        with ExitStack() as ph:
            wds = [ptile(ph, "wds%d" % i, [128, NJ * 128]) for i in range(2)]
            wd = [ptile(ph, "wd%d" % i, [128, NJ * 128], BF16) for i in range(2)]
            abk = [ptile(ph, "abk%d" % i, [128, NJ * 512], BF16) for i in range(2)]
            nb = 0
            nw = 0
            for bi, (t0, tn) in enumerate(TB):
                kind = 1 if t0 < NCTX else 0
                ab_t = abk[bi % 2]
                region = ab_t.t[:].rearrange("p (j t) -> p j t", j=NJ)
                for j in range(NJ):
                    P.dma("sp", ab_t.v(region[:, j, 0:tn]), View(act_scr[j, :, t0:t0 + tn], [act_scr_b[j][bi]]),
                          partial=(j > 0))
                for i in range(KD):
                    w = wd[nw % 2]
                    load_w(w, fwdn_d[l, i].rearrange("p j f -> p (j f)"), wds[nw % 2], NJ * 128)
                    nw += 1
                    ps = PS[nb % 4]
                    nb += 1
                    for j in range(NJ):
                        P.mm(ps[:, 0:tn], w[:, j * 128:(j + 1) * 128], ab_t.v(region[:, j, 0:tn]),
                             start=(j == 0), stop=(j == NJ - 1))
                    P.stt("dve", xTv(i, t0, tn), ps[:, 0:tn], cf(5, i, kind), xTv(i, t0, tn), ALU.mult, ALU.add,
                          partial=True)
            P.barrier()

    triF = sb("triF", [128, 128])
    triB = sb("triB", [128, 128])
    ones128 = sb("ones128", [128, 128])
    negiF = sb("negiF", [128, 128])
    negiB = sb("negiB", [128, 128])
    possF = sb("possF", [128, 128])
    possB = sb("possB", [128, 128])
    P.memset("pool", ones128[:], 1.0)

    def aff(tile, val, step, cm, op, fill):
        P.memset("pool", tile[:], val)
        P.op("pool", lambda: nc.gpsimd.affine_select(out=tile.t[:], in_=tile.t[:], pattern=[[step, 128]],
                                                      compare_op=op, fill=fill, base=0, channel_multiplier=cm),
             [tile[:]], [tile[:]])

    aff(triF, 1.0, 1, -1, ALU.is_ge, 0.0)
    aff(triB, 1.0, -1, 1, ALU.is_ge, 0.0)
    aff(negiF, 0.0, 1, -1, ALU.is_ge, -BIG)
    aff(negiB, 0.0, -1, 1, ALU.is_ge, -BIG)
    aff(possF, 0.0, -1, 1, ALU.is_gt, BIG)
    aff(possB, 0.0, 1, -1, ALU.is_gt, BIG)

    if mixer_on:
        qT_s = dscr("qT_s", dump=True, shape=[1024, NT])
        kT_s = dscr("kT_s", dump=True, shape=[1024, NT])
        ktok_s = dscr("ktok_s", dump=True, shape=[NT, 1024])
        vtok_s = dscr("vtok_s", dump=True, shape=[NT, 1024])
        z_s = dscr("z_s", dump=True, shape=[NT, 1024])
        gates_s = dscr("gates_s", dump=True, shape=[NT, 32])
        yf_s = dscr("yf_s", dump=True, shape=[NT, 1024])
        ucv_s = dscr("ucv_s", dump=True, shape=[4, 128, NT])
    llist = cfg.get("layer_list", list(range(nlayers)))
    if mixer_on and any(l_ % 2 == 0 for l_ in llist):
        ewfm_d = din("e_w_fm", [2, 16, 128, KD, 128])
        ewtm_d = din("e_w_tm", [2, 6, 128, KD, 256])
        ewg_d = din("e_w_g", [2, 128, KD, 16])
        ebfm_d = din("e_b_fm", [2, 128, 16])
        ebtm_d = din("e_b_tm", [2, 128, 1552])
        ehg_d = din("e_head_g_bc", [2, 128, 512])
        ecdw_d = din("e_cdw", [2, 128, 4, 31])
        ecdwb_d = din("e_cdwb", [2, 128, 4])
        elng_d = din("e_lng", [2, 128, 4])
        elnb_d = din("e_lnb", [2, 128, 4])
        ewout_d = din("e_w_out", [2, KD, 128, KD, 128])

    def out_proj(w_d):
        with ExitStack() as ph:
            wbs = [ptile(ph, "wos%d" % i, [128, KD * 128]) for i in range(2)]
            wb = [ptile(ph, "wo%d" % i, [128, KD * 128], BF16) for i in range(2)]
            n = 0
            for i in range(KD):
                w = wb[i % 2]
                load_w(w, w_d[i].rearrange("p k f -> p (k f)"), wbs[i % 2], KD * 128)
                for bi, (t0, tn) in enumerate(TB):
                    kind = 1 if t0 < NCTX else 0
                    ps = PS[n % 4]
                    n += 1
                    for kc in range(KD):
                        P.mm(ps[:, 0:tn], w[:, kc * 128:(kc + 1) * 128], aTv(kc, t0, tn),
                             start=(kc == 0), stop=(kc == KD - 1))
                    P.stt("dve", xTv(i, t0, tn), ps[:, 0:tn], cf(2, i, kind), xTv(i, t0, tn), ALU.mult, ALU.add,
                          partial=True)
            P.barrier()

    def even_inproj(j):
        SC = 128.0 ** -0.5
        with ExitStack() as ph:
            ebfm = ptile(ph, "ebfm", [128, 16])
            P.dma("pool", ebfm[:], View(ebfm_d[j], []))
            ebq = ptile(ph, "ebq", [128, 4])
            P.ts("dve", ebq[:], ebfm[:, 0:4], SC, None, ALU.mult)
            cdw = ptile(ph, "cdw", [128, 4 * 31])
            cdwb = ptile(ph, "cdwb", [128, 4])
            P.dma("pool", cdw[:], View(ecdw_d[j].rearrange("p i k -> p (i k)"), []))
            P.dma("pool", cdwb[:], View(ecdwb_d[j], []))
            with ExitStack() as ph2:
                wbs = [ptile(ph2, "ewfs%d" % i, [128, KD * 128]) for i in range(2)]
                wb = [ptile(ph2, "ewf%d" % i, [128, KD * 128], BF16) for i in range(2)]
                st = [ptile(ph2, "est%d" % i, [128, NT]) for i in range(2)]
                uu = ptile(ph2, "euu", [128, NT])
                cv = ptile(ph2, "ecv", [128, NT])
                npb = 0
                for blk in range(16):
                    w = wb[blk % 2]
                    load_w(w, ewfm_d[j, blk].rearrange("p k f -> p (k f)"), wbs[blk % 2], KD * 128)
                    s = st[blk % 2]
                    for bi, (t0, tn) in enumerate(TB):
                        ps = PS[npb % 4]
                        npb += 1
                        for kc in range(KD):
                            P.mm(ps[:, 0:tn], w[:, kc * 128:(kc + 1) * 128], aTv(kc, t0, tn),
                                 start=(kc == 0), stop=(kc == KD - 1))
                        if blk < 4:
                            P.act(s[:, t0:t0 + tn], ps[:, 0:tn], AF.Identity, bias=ebq[:, blk:blk + 1], scale=SC,
                                  partial=True)
                        elif blk < 8 or blk % 2 == 0:
                            P.act(s[:, t0:t0 + tn], ps[:, 0:tn], AF.Identity, bias=ebfm[:, blk:blk + 1], scale=1.0,
                                  partial=True)
                        else:
                            P.act(s[:, t0:t0 + tn], ps[:, 0:tn], AF.Sigmoid, bias=ebfm[:, blk:blk + 1], scale=1.0,
                                  partial=True)
                    if blk < 4:
                        P.dma("pool", View(qT_s[blk * 128:(blk + 1) * 128, :], []), s[:])
                    elif blk < 8:
                        P.dma("pool", View(kT_s[(blk - 4) * 128:(blk - 3) * 128, :], []), s[:])
                    elif blk % 2 == 1:
                        i = (blk - 8) // 2
                        a_st = st[(blk - 1) % 2]
                        P.tt("dve", uu[:], a_st[:], s[:], ALU.mult)

                        def wk(k):
                            return cdw[:, i * 31 + k:i * 31 + k + 1]
                        P.ts("dve", cv[:], uu[:], wk(15), None, ALU.mult)
                        for k in range(31):
                            if k == 15:
                                continue
                            off = k - 15
                            for (s0, s1) in ((0, NCTX), (NCTX, NT)):
                                a = max(s0, s0 - off)
                                b = min(s1, s1 - off)
                                if b <= a:
                                    continue
                                P.stt("dve", cv[:, a:b], uu[:, a + off:b + off], wk(k), cv[:, a:b], ALU.mult, ALU.add,
                                      partial=True)
                        P.ts("dve", cv[:], cv[:], cdwb[:, i:i + 1], None, ALU.add)
                        P.dma("pool", View(ucv_s[i], []), cv[:])
                P.barrier()
            with ExitStack() as ph2:
                wts = [ptile(ph2, "ewts%d" % i, [128, KD * 256]) for i in range(2)]
                wt = [ptile(ph2, "ewt%d" % i, [128, KD * 256], BF16) for i in range(2)]
                wgs_ = ptile(ph2, "ewgs", [128, KD * 16])
                wg_ = ptile(ph2, "ewg", [128, KD * 16], BF16)
                ebtm = ptile(ph2, "ebtm", [128, 1552])
                P.dma("pool", ebtm[:], View(ebtm_d[j], []))
                stg = [ptile(ph2, "etm%d" % i, [128, 256]) for i in range(3)]
                gst = [ptile(ph2, "egs%d" % i, [128, 16]) for i in range(2)]
                gtmp = [ptile(ph2, "egt%d" % i, [128, 4]) for i in range(2)]
                n = 0
                for g in range(6):
                    w = wt[g % 2]
                    load_w(w, ewtm_d[j, g].rearrange("p k f -> p (k f)"), wts[g % 2], KD * 256)
                    dst = (ktok_s, vtok_s, z_s)[g // 2]
                    c0 = (g % 2) * 256
                    for ti in range(NTILE):
                        ps = PS[n % 4]
                        s = stg[n % 3]
                        n += 1
                        for kc in range(KD):
                            P.mm(ps[:, 0:256], aTv(kc, ti * 128, 128), w[:, kc * 256:(kc + 1) * 256],
                                 start=(kc == 0), stop=(kc == KD - 1))
                        P.tt("dve", s[:], ps[:, 0:256], ebtm[:, g * 256:(g + 1) * 256], ALU.add)
                        if g >= 4:
                            P.act(s[:], s[:], AF.Sigmoid)
                        P.dma("pool", View(dst[ti * 128:(ti + 1) * 128, c0:c0 + 256], []), s[:])
                load_w(wg_, ewg_d[j].rearrange("p k f -> p (k f)"), wgs_, KD * 16)
                for ti in range(NTILE):
                    ps = PS[4 + ti % 2]
                    s = gst[ti % 2]
                    tmp = gtmp[ti % 2]
                    for kc in range(KD):
                        P.mm(ps[:, 0:16], aTv(kc, ti * 128, 128), wg_[:, kc * 16:(kc + 1) * 16],
                             start=(kc == 0), stop=(kc == KD - 1))
                    P.tt("dve", s[:], ps[:, 0:16], ebtm[:, 1536:1552], ALU.add)
                    for c0 in (4, 12):
                        P.act(tmp[:], s[:, c0:c0 + 4], AF.Exp, scale=-1.0)
                        P.act(tmp[:], tmp[:], AF.Ln, bias=1.0, scale=1.0)
                        P.ts("dve", s[:, c0:c0 + 4], tmp[:], -1.0, None, ALU.mult, partial=True)
                    P.dma("pool", View(gates_s[ti * 128:(ti + 1) * 128, 0:16], []), s[:])
            P.barrier()

    def conformer_ln(j):
        with ExitStack() as ph:
            lng = ptile(ph, "lng", [128, 4])
            lnb = ptile(ph, "lnb", [128, 4])
            P.dma("pool", lng[:], View(elng_d[j], []))
            P.dma("pool", lnb[:], View(elnb_d[j], []))
            onesC = ptile(ph, "onesC", [128, 128])
            P.memset("pool", onesC[:], 1.0 / 512)
            cw = [ptile(ph, "cw%d" % i, [128, NT]) for i in range(4)]
            for i in range(4):
                P.dma("sp", cw[i][:], View(ucv_s[i], []))
            sq = [ptile(ph, "csq%d" % i, [128, 512]) for i in range(2)]
            mt = [ptile(ph, "cmt%d" % i, [128, 512]) for i in range(2)]
            rt = [ptile(ph, "crt%d" % i, [128, 512]) for i in range(2)]
            for bi, (t0, tn) in enumerate(TB):
                psm = PS[4 + bi % 2]
                for i in range(4):
                    P.mm(psm[:, 0:tn], onesC[:], cw[i][:, t0:t0 + tn], start=(i == 0), stop=(i == 3))
                mean = mt[bi % 2]
                P.copy("act", mean[:, 0:tn], psm[:, 0:tn])
                for i in range(4):
                    P.tt("dve", cw[i][:, t0:t0 + tn], cw[i][:, t0:t0 + tn], mean[:, 0:tn], ALU.subtract, partial=True)
                psv = PS[6 + bi % 2]
                for i in range(4):
                    s = sq[i % 2]
                    P.act(s[:, 0:tn], cw[i][:, t0:t0 + tn], AF.Square)
                    P.mm(psv[:, 0:tn], onesC[:], s[:, 0:tn], start=(i == 0), stop=(i == 3))
                r = rt[bi % 2]
                P.act(r[:, 0:tn], psv[:, 0:tn], AF.Sqrt, bias=EPS, scale=1.0)
                P.recip(r[:, 0:tn], r[:, 0:tn])
                for i in range(4):
                    P.tt("dve", cw[i][:, t0:t0 + tn], cw[i][:, t0:t0 + tn], r[:, 0:tn], ALU.mult, partial=True)
                    P.act(aTv(4 + i, t0, tn), cw[i][:, t0:t0 + tn], AF.Silu, bias=lnb[:, i:i + 1],
                          scale=lng[:, i:i + 1], partial=True)
            P.barrier()

    def mlstm_pass(j, direction):
        fwd = direction == 0
        tri = triF if fwd else triB
        negi = negiF if fwd else negiB
        base = 0 if fwd else 8
        order = list(range(NTILE)) if fwd else [1, 0] + list(range(NTILE - 1, 1, -1))
        with ExitStack() as ph:
            Cx = [ptile(ph, "Cx%d" % h, [128, 129]) for h in range(4)]
            for h in range(4):
                P.memset("pool", Cx[h][:], 0.0)
            qTb = [ptile(ph, "mq%d" % i, [128, 512]) for i in range(2)]
            kTb = [ptile(ph, "mk%d" % i, [128, 512]) for i in range(2)]
            ktb = [ptile(ph, "mkt%d" % i, [128, 512]) for i in range(2)]
            vxb = [ptile(ph, "mvx%d" % i, [128, 4 * 129]) for i in range(2)]
            gtb = [ptile(ph, "mg%d" % i, [128, 16]) for i in range(2)]
            for vx in vxb:
                P.memset("pool", vx[:], 1.0)
            if not fwd:
                yfb = [ptile(ph, "myf%d" % i, [128, 512]) for i in range(2)]
                ogb = [ptile(ph, "mog%d" % i, [128, 512]) for i in range(2)]
                hgbc = ptile(ph, "hgbc", [128, 512])
                P.dma("pool", hgbc[:], View(ehg_d[j], []))
                ysq = ptile(ph, "ysq", [128, 512])
                gg = ptile(ph, "gg", [128, 512])
                ssb = [ptile(ph, "ss%d" % i, [128, 4]) for i in range(2)]
            G2 = [ptile(ph, "G2%d" % i, [128, 512]) for i in range(2)]
            eBB = [ptile(ph, "eBB%d" % i, [128, 512]) for i in range(2)]
            Bcolb = [ptile(ph, "Bc%d" % i, [128, 4]) for i in range(2)]
            b1b = [ptile(ph, "b1%d" % i, [128, 4]) for i in range(2)]
            wlb = [ptile(ph, "wl%d" % i, [128, 4]) for i in range(2)]
            eBlb = [ptile(ph, "eBl%d" % i, [128, 4]) for i in range(2)]
            tmpTb = [ptile(ph, "tT%d" % i, [128, 128]) for i in range(2)]
            DmTb = [ptile(ph, "DmT%d" % i, [128, 128]) for i in range(2)]
            PTb = [ptile(ph, "PT%d" % i, [128, 128]) for i in range(2)]
            qbTb = [ptile(ph, "qbT%d" % i, [128, 128]) for i in range(2)]
            kwb = [ptile(ph, "kw%d" % i, [128, 128]) for i in range(2)]
            ystb = [ptile(ph, "yst%d" % i, [128, 512]) for i in range(2)]
            rdb = [ptile(ph, "rd%d" % i, [128, 1]) for i in range(2)]
            psBB, psCol, psCol2 = PS[0], PS[1], PS[7]
            psST = [PS[2], PS[3]]
            psNum = [PS[4], PS[5]]
            psC = PS[6]

            def load(ci):
                r = ci % 2
                c = order[ci]
                t0 = c * 128
                P.dma("sp", qTb[r].v(qTb[r].t[:].rearrange("p (h t) -> p h t", h=4)),
                      View(qT_s[0:512, t0:t0 + 128].rearrange("(h p) t -> p h t", p=128), []))
                P.dma("sp", kTb[r].v(kTb[r].t[:].rearrange("p (h t) -> p h t", h=4)),
                      View(kT_s[0:512, t0:t0 + 128].rearrange("(h p) t -> p h t", p=128), []))
                P.dma("sp", ktb[r][:], View(ktok_s[t0:t0 + 128, 0:512], []))
                P.dma("sp", vxb[r].v(vxb[r].t[:].rearrange("p (h d) -> p h d", h=4)[:, :, 0:128]),
                      View(vtok_s[t0:t0 + 128, 0:512].rearrange("p (h d) -> p h d", h=4), []), partial=True)
                P.dma("sp", gtb[r][:], View(gates_s[t0:t0 + 128, 0:16], []))
                if not fwd:
                    P.dma("sp", yfb[r][:], View(yf_s[t0:t0 + 128, 0:512], []))
                    P.dma("sp", ogb[r][:], View(z_s[t0:t0 + 128, 0:512], []))

            load(0)
            nu = 0
            for ci, c in enumerate(order):
                r = ci % 2
                t0 = c * 128
                if ci + 1 < len(order):
                    load(ci + 1)
                qTr, kTr, ktr, vx, gt = qTb[r], kTb[r], ktb[r], vxb[r], gtb[r]
                G2r, eBBr, Bcol, b1, wl, eBl = G2[r], eBB[r], Bcolb[r], b1b[r], wlb[r], eBlb[r]
                lf = gt[:, base + 4:base + 8]
                li = gt[:, base:base + 4]
                P.tt("dve", G2r.v(G2r.t[:].rearrange("p (h s) -> p h s", h=4)),
                     tri.v(tri.t[:].unsqueeze(1).to_broadcast([128, 4, 128])),
                     gt.v(gt.t[:, base + 4:base + 8].unsqueeze(2).to_broadcast([128, 4, 128])), ALU.mult)
                P.mm(psBB[:, 0:512], ones128[:], G2r[:], True, True)
                P.mm(psCol[:, 0:4], tri[:], lf, True, True)
                P.mm(psCol2[:, 0:4], ones128[:], lf, True, True)
                P.copy("act", Bcol[:], psCol[:, 0:4])
                P.tt("dve", b1[:], li, psCol[:, 0:4], ALU.subtract)
                P.tt("dve", wl[:], b1[:], psCol2[:, 0:4], ALU.add)
                P.act(wl[:], wl[:], AF.Exp)
                P.act(eBl[:], psCol2[:, 0:4], AF.Exp)
                P.act(eBBr[:], psBB[:, 0:512], AF.Exp)
                yst = ystb[r]
                for h in range(4):
                    u = nu % 2
                    nu += 1
                    hs = slice(h * 128, (h + 1) * 128)
                    psS = psST[u]
                    psN = psNum[u]
                    tmpT, DmT, PT, qbT, kw, rden = tmpTb[u], DmTb[u], PTb[u], qbTb[u], kwb[u], rdb[u]
                    P.mm(psS[:, 0:128], kTr[:, hs], qTr[:, hs], True, True)
                    P.stt("dve", tmpT[:], psBB[:, hs], Bcol[:, h:h + 1], negi[:], ALU.min, ALU.add)
                    P.act(DmT[:], tmpT[:], AF.Exp, bias=b1[:, h:h + 1], scale=1.0)
                    P.tt("dve", PT[:], psS[:, 0:128], DmT[:], ALU.mult)
                    P.tt("dve", qbT[:], qTr[:, hs], eBBr[:, hs], ALU.mult)
                    P.mm(psN[:, 0:129], qbT[:], Cx[h][:], True, False)
                    P.mm(psN[:, 0:129], PT[:], vx[:, h * 129:(h + 1) * 129], False, True)
                    P.ts("dve", rden[:], psN[:, 128:129], -1.0, None, ALU.mult)
                    P.stt("dve", rden[:], psN[:, 128:129], 1.0, rden[:], ALU.max, ALU.max)
                    P.recip(rden[:], rden[:])
                    if fwd:
                        P.act(yst[:, hs], psN[:, 0:128], AF.Identity, bias=0.0, scale=rden[:, 0:1], partial=True)
                    else:
                        P.stt("dve", yst[:, hs], psN[:, 0:128], rden[:, 0:1], yfb[r][:, hs], ALU.mult, ALU.add,
                              partial=True)
                    P.act(kw[:], ktr[:, hs], AF.Identity, bias=0.0, scale=wl[:, h:h + 1])
                    P.mm(psC[:, 0:129], kw[:], vx[:, h * 129:(h + 1) * 129], True, True)
                    P.stt("dve", Cx[h][:], Cx[h][:], eBl[:, h:h + 1], psC[:, 0:129], ALU.mult, ALU.add)
                if fwd:
                    P.dma("pool", View(yf_s[t0:t0 + 128, 0:512], []), yst[:])
                else:
                    ss = ssb[r]
                    P.tt("dve", ysq[:], yst[:], yst[:], ALU.mult)
                    P.op("dve", lambda: nc.vector.tensor_reduce(
                        out=ss.t[:], in_=ysq.t[:].rearrange("p (h d) -> p h d", h=4), axis=AX.X, op=ALU.add),
                        [ss[:]], [ysq[:]])
                    P.act(ss[:], ss[:], AF.Sqrt, bias=EPS, scale=1.0 / 128)
                    P.recip(ss[:], ss[:])
                    P.tt("dve", gg[:], ogb[r][:], hgbc[:], ALU.mult)
                    y3 = yst.v(yst.t[:].rearrange("p (h d) -> p h d", h=4))
                    P.tt("dve", y3, y3, ss.v(ss.t[:].unsqueeze(2).to_broadcast([128, 4, 128])), ALU.mult)
                    P.tt("dve", yst[:], yst[:], gg[:], ALU.mult)
                    psT = psBB
                    for h in range(4):
                        P.tr(psT[:, h * 128:(h + 1) * 128], yst[:, h * 128:(h + 1) * 128], ident[:], partial=(h > 0))
                    P.copy("act", View(aT3[:, 0:4, t0:t0 + 128], aT_b[0:4]),
                           psT.v(psT.t[:].rearrange("p (h t) -> p h t", h=4)), partial=True)
            P.barrier()

    def even_layer(l):
        j = l // 2
        norm_phase(lambda kc, kind: cf(0, kc, kind), lambda kc, kind: cf(1, kc, kind), TB)
        even_inproj(j)
        conformer_ln(j)
        mlstm_pass(j, 0)
        mlstm_pass(j, 1)
        if dbg:
            dd = nc.dram_tensor("dbg_aT%d" % l, [128, KD * NT], BF16, kind="ExternalOutput").ap()
            dbg_names.append("dbg_aT%d" % l)
            P.dma("sp", View(dd, []), View(aT.t[:], aT_b))
            P.barrier()
        out_proj(ewout_d[j])

    class Sub:
        def __init__(self, tile, c0, n):
            self.t = tile.t
            self.c0 = c0
            self.n = n
            self.buf = tile.buf

        def v(self, a=0, b=None):
            b = self.n if b is None else b
            return View(self.t[:, self.c0 + a:self.c0 + b], [self.buf], True)

    if mixer_on and cfg.get("odd", True) and any(l_ % 2 == 1 for l_ in llist):
        owfm_d = din("o_w_fm", [2, 24, 128, KD, 128])
        owtm_d = din("o_w_tm", [2, 4, 128, KD, 256])
        owg_d = din("o_w_g", [2, 128, KD, 32])
        oshort_d = din("o_short", [2, 128, 24, 5])
        oalog_d = din("o_alog_bc", [2, 128, 16])
        odtb_d = din("o_dtb_bc", [2, 128, 16])
        ohg_d = din("o_head_g_bc", [2, 128, 128])
        owout_d = din("o_w_out", [2, KD, 128, KD, 128])

    def odd_inproj(j):
        SC = 128.0 ** -0.5
        with ExitStack() as ph:
            shw = ptile(ph, "shw", [128, 24 * 5])
            P.dma("pool", shw[:], View(oshort_d[j].rearrange("p b k -> p (b k)"), []))
            with ExitStack() as ph2:
                wbs = [ptile(ph2, "owfs%d" % i, [128, KD * 128]) for i in range(2)]
                wb = [ptile(ph2, "owf%d" % i, [128, KD * 128], BF16) for i in range(2)]
                st = [ptile(ph2, "ost%d" % i, [128, NT]) for i in range(1)]
                cvb = [ptile(ph2, "ocv%d" % i, [128, NT]) for i in range(2)]
                tks = ptile(ph2, "otk", [128, NTILE * 128])
                sqb = [ptile(ph2, "osq%d" % i, [128, 512]) for i in range(1)]
                rb = [ptile(ph2, "orr%d" % i, [128, 512]) for i in range(2)]
                npb = 0
                for blk in range(24):
                    w = wb[blk % 2]
                    load_w(w, owfm_d[j, blk].rearrange("p k f -> p (k f)"), wbs[blk % 2], KD * 128)
                    s = st[0]
                    cv = cvb[blk % 2]
                    for bi, (t0, tn) in enumerate(TB):
                        ps = PS[npb % 4]
                        npb += 1
                        for kc in range(KD):
                            P.mm(ps[:, 0:tn], w[:, kc * 128:(kc + 1) * 128], aTv(kc, t0, tn),
                                 start=(kc == 0), stop=(kc == KD - 1))
                        P.copy("act", s[:, t0:t0 + tn], ps[:, 0:tn], partial=True)

                    def wk(k):
                        return shw[:, blk * 5 + k:blk * 5 + k + 1]
                    P.ts("dve", cv[:], s[:], wk(2), None, ALU.mult)
                    for k in range(5):
                        if k == 2:
                            continue
                        off = k - 2
                        for (s0, s1) in ((0, NCTX), (NCTX, NT)):
                            a = max(s0, s0 - off)
                            b = min(s1, s1 - off)
                            P.stt("dve", cv[:, a:b], s[:, a + off:b + off], wk(k), cv[:, a:b], ALU.mult, ALU.add,
                                  partial=True)
                    P.act(cv[:], cv[:], AF.Silu)
                    if blk < 16:
                        for bi, (t0, tn) in enumerate(TB):
                            sq = sqb[0]
                            r = rb[bi % 2]
                            ps = PS[4 + bi % 2]
                            P.act(sq[:, 0:tn], cv[:, t0:t0 + tn], AF.Square)
                            P.mm(ps[:, 0:tn], ones128[:], sq[:, 0:tn], True, True)
                            P.act(r[:, 0:tn], ps[:, 0:tn], AF.Sqrt, bias=EPS, scale=1.0)
                            P.recip(r[:, 0:tn], r[:, 0:tn])
                            P.stt("dve", cv[:, t0:t0 + tn], cv[:, t0:t0 + tn], SC if blk < 8 else 1.0, r[:, 0:tn],
                                  ALU.mult, ALU.mult, partial=True)
                        dst = qT_s if blk < 8 else kT_s
                        hb = blk % 8
                        P.dma("pool", View(dst[hb * 128:(hb + 1) * 128, :], []), cv[:])
                    if blk >= 8:
                        hb = blk % 8
                        for q4 in range(0, NTILE, 4):
                            ps = PS[6 + (q4 // 4) % 2]
                            nq = min(4, NTILE - q4)
                            for q in range(nq):
                                ti = q4 + q
                                P.tr(ps[:, q * 128:(q + 1) * 128], cv[:, ti * 128:(ti + 1) * 128], ident[:],
                                     partial=(q > 0))
                            P.copy("act" if (q4 // 4) % 2 == 0 else "dve", tks[:, q4 * 128:(q4 + nq) * 128],
                                   ps[:, 0:nq * 128], partial=True)
                        dst = ktok_s if blk < 16 else vtok_s
                        P.dma("pool", View(dst.rearrange("(n p) f -> p n f", p=128)[:, :, hb * 128:(hb + 1) * 128], []),
                              tks.v(tks.t[:].rearrange("p (n f) -> p n f", f=128)))
                P.barrier()
            with ExitStack() as ph2:
                wts = [ptile(ph2, "owts%d" % i, [128, KD * 256]) for i in range(2)]
                wt = [ptile(ph2, "owt%d" % i, [128, KD * 256], BF16) for i in range(2)]
                wgs_ = ptile(ph2, "owgs", [128, KD * 32])
                wg_ = ptile(ph2, "owg", [128, KD * 32], BF16)
                alog = ptile(ph2, "oalog", [128, 16])
                dtb = ptile(ph2, "odtb", [128, 16])
                P.dma("pool", alog[:], View(oalog_d[j], []))
                P.dma("pool", dtb[:], View(odtb_d[j], []))
                P.act(alog[:], alog[:], AF.Exp)
                P.ts("dve", alog[:], alog[:], -1.0, None, ALU.mult)
                stg = [ptile(ph2, "otm%d" % i, [128, 256]) for i in range(3)]
                gst = [ptile(ph2, "ogs%d" % i, [128, 32]) for i in range(2)]
                gtmp = [ptile(ph2, "ogt%d" % i, [128, 8]) for i in range(2)]
                n = 0
                for g in range(4):
                    w = wt[g % 2]
                    load_w(w, owtm_d[j, g].rearrange("p k f -> p (k f)"), wts[g % 2], KD * 256)
                    for ti in range(NTILE):
                        ps = PS[n % 4]
                        s = stg[n % 3]
                        n += 1
                        for kc in range(KD):
                            P.mm(ps[:, 0:256], aTv(kc, ti * 128, 128), w[:, kc * 256:(kc + 1) * 256],
                                 start=(kc == 0), stop=(kc == KD - 1))
                        P.act(s[:], ps[:, 0:256], AF.Silu)
                        P.dma("pool", View(z_s[ti * 128:(ti + 1) * 128, g * 256:(g + 1) * 256], []), s[:])
                load_w(wg_, owg_d[j].rearrange("p k f -> p (k f)"), wgs_, KD * 32)
                for ti in range(NTILE):
                    ps = PS[4 + ti % 2]
                    s = gst[ti % 2]
                    tmp = gtmp[ti % 2]
                    for kc in range(KD):
                        P.mm(ps[:, 0:32], aTv(kc, ti * 128, 128), wg_[:, kc * 32:(kc + 1) * 32],
                             start=(kc == 0), stop=(kc == KD - 1))
                    for d_ in range(2):
                        c0 = d_ * 16
                        P.tt("dve", tmp[:], ps[:, c0:c0 + 8], dtb[:, d_ * 8:(d_ + 1) * 8], ALU.add)
                        P.act(tmp[:], tmp[:], AF.Exp)
                        P.act(tmp[:], tmp[:], AF.Ln, bias=1.0, scale=1.0)
                        P.tt("dve", s[:, c0:c0 + 8], tmp[:], alog[:, d_ * 8:(d_ + 1) * 8], ALU.mult, partial=True)
                        P.act(s[:, c0 + 8:c0 + 16], ps[:, c0 + 8:c0 + 16], AF.Sigmoid, partial=True)
                    P.dma("pool", View(gates_s[ti * 128:(ti + 1) * 128, 0:32], []), s[:])
                P.barrier()

    def gdn_pass(j, direction):
        fwd = direction == 0
        tri = triF if fwd else triB
        poss = possF if fwd else possB
        negi = negiF if fwd else negiB
        base = 0 if fwd else 16
        order = list(range(NTILE)) if fwd else [1, 0] + list(range(NTILE - 1, 1, -1))
        steps = [(c, hg) for c in order for hg in range(2)]
        steps = steps[:cfg.get("gdn_max_steps", len(steps))]
        with ExitStack() as ph:
            S2 = [[ptile(ph, "S2%d%d" % (h, g), [128, 256]) for g in range(2)] for h in range(2)]
            for h in range(2):
                for g in range(2):
                    P.memset("pool", S2[h][g][:], 0.0)
            qTb = [ptile(ph, "gq%d" % i, [128, 512]) for i in range(2)]
            kTb = [ptile(ph, "gk%d" % i, [128, 512]) for i in range(2)]
            ktb = [ptile(ph, "gkt%d" % i, [128, 512]) for i in range(2)]
            vtb = [ptile(ph, "gvt%d" % i, [128, 512]) for i in range(2)]
            gtb = [ptile(ph, "gg%d" % i, [128, 32]) for i in range(2)]
            if not fwd:
                yfb = [ptile(ph, "gyf%d" % i, [128, 512]) for i in range(2)]
                zsb = [ptile(ph, "gzs%d" % i, [128, 512]) for i in range(2)]
                hgbc = ptile(ph, "ghg", [128, 128])
                P.dma("pool", hgbc[:], View(ohg_d[j], []))
                ssb = [ptile(ph, "gss%d" % i, [128, 4]) for i in range(2)]
            G2b = [ptile(ph, "gG2%d" % i, [128, 512]) for i in range(2)]
            egcBb = [ptile(ph, "gegB%d" % i, [128, 512]) for i in range(2)]
            cnames = ["gcol", "ngcol", "egc", "bg", "bE", "kdw", "eGl"]
            cols = [{nm: ptile(ph, "c%s%d" % (nm, i), [128, 4]) for nm in cnames} for i in range(2)]
            tnames = ["E", "ET", "MT", "P0", "P1", "PT0", "PT1", "XT0", "XT1", "kdec"]
            TT = [{nm: ptile(ph, "t2%s%d" % (nm, g), [128, 256]) for nm in tnames} for g in range(2)]
            ystb = [ptile(ph, "gyst%d" % i, [128, 512]) for i in range(2)]
            psG, psCol = PS[0], PS[1]
            BK = [(PS[2], PS[3], PS[4]), (PS[5], PS[6], PS[7])]
            psT = PS[2]

            def v2(t_, c0=0):
                return t_.v(t_.t[:, c0:c0 + 256].rearrange("p (h s) -> p h s", h=2))

            def cbc2(ct, gi):
                return ct.v(ct.t[:, gi * 2:gi * 2 + 2].unsqueeze(2).to_broadcast([128, 2, 128]))

            def mbc2(m):
                return m.v(m.t[:].unsqueeze(1).to_broadcast([128, 2, 128]))

            def load(si):
                r = si % 2
                c, hg = steps[si]
                t0 = c * 128
                f0 = hg * 512
                P.dma("sp", qTb[r].v(qTb[r].t[:].rearrange("p (h t) -> p h t", h=4)),
                      View(qT_s[f0:f0 + 512, t0:t0 + 128].rearrange("(h p) t -> p h t", p=128), []))
                P.dma("sp", kTb[r].v(kTb[r].t[:].rearrange("p (h t) -> p h t", h=4)),
                      View(kT_s[f0:f0 + 512, t0:t0 + 128].rearrange("(h p) t -> p h t", p=128), []))
                P.dma("sp", ktb[r][:], View(ktok_s[t0:t0 + 128, f0:f0 + 512], []))
                P.dma("sp", vtb[r][:], View(vtok_s[t0:t0 + 128, f0:f0 + 512], []))
                P.dma("sp", gtb[r][:], View(gates_s[t0:t0 + 128, 0:32], []))
                if not fwd:
                    P.dma("sp", yfb[r][:], View(yf_s[t0:t0 + 128, f0:f0 + 512], []))
                    P.dma("sp", zsb[r][:], View(z_s[t0:t0 + 128, f0:f0 + 512], []))

            def prep(si):
                r = si % 2
                c, hg = steps[si]
                gt = gtb[r]
                G2, egcB, cl = G2b[r], egcBb[r], cols[r]
                g0 = base + hg * 4
                gv = gt[:, g0:g0 + 4]
                beta = gt[:, g0 + 8:g0 + 12]
                pc = r * 8
                P.tt("dve", G2.v(G2.t[:].rearrange("p (h s) -> p h s", h=4)),
                     tri.v(tri.t[:].unsqueeze(1).to_broadcast([128, 4, 128])),
                     gt.v(gt.t[:, g0:g0 + 4].unsqueeze(2).to_broadcast([128, 4, 128])), ALU.mult)
                P.mm(psG[:, 0:512], ones128[:], G2[:], True, True)
                P.mm_part(psCol[:, pc:pc + 4], tri[:], gv, True, True, partial=False)
                P.mm_part(psCol[:, pc + 4:pc + 8], ones128[:], gv, True, True, partial=True)
                P.copy("act", cl["gcol"][:], psCol[:, pc:pc + 4])
                P.ts("dve", cl["ngcol"][:], psCol[:, pc:pc + 4], -1.0, None, ALU.mult)
                P.act(cl["egc"][:], psCol[:, pc:pc + 4], AF.Exp)
                P.tt("dve", cl["bg"][:], beta, cl["egc"][:], ALU.mult)
                P.act(cl["bE"][:], beta, AF.Ln)
                P.tt("dve", cl["bE"][:], cl["bE"][:], cl["gcol"][:], ALU.add)
                P.tt("dve", cl["kdw"][:], psCol[:, pc + 4:pc + 8], cl["gcol"][:], ALU.subtract)
                P.act(cl["kdw"][:], cl["kdw"][:], AF.Exp)
                P.act(cl["eGl"][:], psCol[:, pc + 4:pc + 8], AF.Exp)
                P.act(egcB[:], psG[:, 0:512], AF.Exp)

            load(0)
            if len(steps) > 1:
                load(1)
            prep(0)
            GR = range(2)
            for si, (c, hg) in enumerate(steps):
                r = si % 2
                t0 = c * 128
                qTr, kTr, ktr, vtr, gt = qTb[r], kTb[r], ktb[r], vtb[r], gtb[r]
                G2, egcB, cl = G2b[r], egcBb[r], cols[r]
                g0 = base + hg * 4
                yst = ystb[r]
                col = lambda nm, h_: cl[nm][:, h_:h_ + 1]
                HS = lambda hh: slice(hh * 128, (hh + 1) * 128)
                LS = lambda li: slice(li * 128, (li + 1) * 128)
                heads = lambda gi: (2 * gi, 2 * gi + 1)
                for gi in GR:
                    bA, bB, bC = BK[gi]
                    for hh in heads(gi):
                        P.mm(bA[:, LS(hh % 2)], kTr[:, HS(hh)], kTr[:, HS(hh)], True, True)
                        P.mm(bB[:, LS(hh % 2)], kTr[:, HS(hh)], qTr[:, HS(hh)], True, True)
                for gi in GR:
                    T = TT[gi]
                    pg = psG.v(psG.t[:, gi * 256:(gi + 1) * 256].rearrange("p (h s) -> p h s", h=2))
                    P.stt("dve", v2(T["E"]), pg, -1.0, mbc2(poss), ALU.mult, ALU.subtract)
                    P.tt("dve", v2(T["ET"]), pg, mbc2(negi), ALU.add)
                for gi in GR:
                    T = TT[gi]
                    for hh in heads(gi):
                        li = hh % 2
                        P.act(T["E"][:, LS(li)], T["E"][:, LS(li)], AF.Exp, bias=col("bE", hh), scale=1.0, partial=True)
                        P.act(T["ET"][:, LS(li)], T["ET"][:, LS(li)], AF.Exp, bias=col("ngcol", hh), scale=1.0,
                              partial=True)
                for gi in GR:
                    T = TT[gi]
                    bA, bB, bC = BK[gi]
                    P.tt("dve", T["P0"][:], bA[:, 0:256], T["E"][:], ALU.mult)
                    P.tt("dve", T["MT"][:], bB[:, 0:256], T["ET"][:], ALU.mult)
                for gi in GR:
                    T = TT[gi]
                    bA, bB, bC = BK[gi]
                    for li in range(2):
                        P.tr(bC[:, LS(li)], T["P0"][:, LS(li)], ident[:], partial=(li > 0))
                for gi in GR:
                    T = TT[gi]
                    bA, bB, bC = BK[gi]
                    P.copy("act", T["PT0"][:], bC[:, 0:256])
                    P.stt("dve", v2(T["XT0"]), bC.v(bC.t[:, 0:256].rearrange("p (h s) -> p h s", h=2)), -1.0,
                          mbc2(ident), ALU.mult, ALU.add)
                for k in range(1, 8):
                    pv, cu = (k - 1) % 2, k % 2
                    for gi in GR:
                        T = TT[gi]
                        bA, bB, bC = BK[gi]
                        Pp, PTp = T["P%d" % pv], T["PT%d" % pv]
                        for li in range(2):
                            if k <= 6:
                                P.mm(bA[:, LS(li)], PTp[:, LS(li)], Pp[:, LS(li)], True, True)
                            if k < 6:
                                P.mm(bB[:, LS(li)], Pp[:, LS(li)], PTp[:, LS(li)], True, True)
                            if k >= 2:
                                P.mm(bC[:, LS(li)], Pp[:, LS(li)], T["XT%d" % cu][:, LS(li)], True, True)
                    for gi in GR:
                        T = TT[gi]
                        bA, bB, bC = BK[gi]
                        if k <= 6:
                            P.copy("act", T["P%d" % cu][:], bA[:, 0:256])
                        if k >= 2:
                            P.tt("dve", T["XT%d" % pv][:], bC[:, 0:256], T["XT%d" % cu][:], ALU.add)
                        if k < 6:
                            P.copy("act" if (k + gi) % 2 == 0 else "dve", T["PT%d" % cu][:], bB[:, 0:256])
                    if k == 3 and si + 1 < len(steps):
                        prep(si + 1)
                for gi in GR:
                    T = TT[gi]
                    c0 = gi * 256
                    P.tt("dve", v2(T["E"]), v2(vtr, c0),
                         gt.v(gt.t[:, g0 + 8 + 2 * gi:g0 + 10 + 2 * gi].unsqueeze(2).to_broadcast([128, 2, 128])),
                         ALU.mult)
                    P.tt("dve", v2(T["ET"]), v2(ktr, c0), cbc2(cl["bg"], gi), ALU.mult)
                    P.tt("dve", v2(T["kdec"]), v2(ktr, c0), cbc2(cl["kdw"], gi), ALU.mult)
                    P.tt("dve", T["PT1"][:], qTr[:, c0:c0 + 256], egcB[:, c0:c0 + 256], ALU.mult)
                for gi in GR:
                    T = TT[gi]
                    bA, bB, bC = BK[gi]
                    for li in range(2):
                        P.mm(bA[:, LS(li)], T["XT0"][:, LS(li)], T["E"][:, LS(li)], True, True)
                        P.mm(bB[:, LS(li)], T["ET"][:, LS(li)], T["XT0"][:, LS(li)], True, True)
                for gi in GR:
                    T = TT[gi]
                    bA, bB, bC = BK[gi]
                    P.copy("act", T["P0"][:], bB[:, 0:256])
                    P.copy("act", T["P1"][:], bA[:, 0:256])
                for gi in GR:
                    T = TT[gi]
                    bA, bB, bC = BK[gi]
                    S = S2[hg][gi]
                    for li in range(2):
                        P.mm(bC[:, LS(li)], T["P0"][:, LS(li)], S[:, LS(li)], True, True)
                for gi in GR:
                    T = TT[gi]
                    bA, bB, bC = BK[gi]
                    P.tt("dve", T["PT0"][:], T["P1"][:], bC[:, 0:256], ALU.subtract)
                for gi in GR:
                    T = TT[gi]
                    bA, bB, bC = BK[gi]
                    S = S2[hg][gi]
                    for li in range(2):
                        P.mm(bB[:, LS(li)], T["PT1"][:, LS(li)], S[:, LS(li)], True, False)
                        P.mm(bB[:, LS(li)], T["MT"][:, LS(li)], T["PT0"][:, LS(li)], False, True)
                    for li in range(2):
                        P.mm(bA[:, LS(li)], T["kdec"][:, LS(li)], T["PT0"][:, LS(li)], True, True)
                for gi in GR:
                    T = TT[gi]
                    bA, bB, bC = BK[gi]
                    S = S2[hg][gi]
                    c0 = gi * 256
                    if fwd:
                        P.copy("act", yst[:, c0:c0 + 256], bB[:, 0:256], partial=True)
                    else:
                        P.tt("dve", yst[:, c0:c0 + 256], bB[:, 0:256], yfb[r][:, c0:c0 + 256], ALU.add, partial=True)
                    P.tt("dve", v2(S), v2(S), cbc2(cl["eGl"], gi), ALU.mult)
                    P.tt("dve", S[:], S[:], bA[:, 0:256], ALU.add)
                if fwd:
                    P.dma("pool", View(yf_s[t0:t0 + 128, hg * 512:(hg + 1) * 512], []), yst[:])
                else:
                    ss = ssb[r]
                    ysq = G2
                    P.tt("dve", ysq[:], yst[:], yst[:], ALU.mult)
                    P.op("dve", lambda: nc.vector.tensor_reduce(
                        out=ss.t[:], in_=ysq.t[:].rearrange("p (h d) -> p h d", h=4), axis=AX.X, op=ALU.add),
                        [ss[:]], [ysq[:]])
                    P.act(ss[:], ss[:], AF.Sqrt, bias=EPS, scale=1.0 / 128)
                    P.recip(ss[:], ss[:])
                    y3 = yst.v(yst.t[:].rearrange("p (h d) -> p h d", h=4))
                    z3 = zsb[r].v(zsb[r].t[:].rearrange("p (h d) -> p h d", h=4))
                    P.tt("dve", z3, z3, hgbc.v(hgbc.t[:].unsqueeze(1).to_broadcast([128, 4, 128])), ALU.mult)
                    P.tt("dve", y3, y3, ss.v(ss.t[:].unsqueeze(2).to_broadcast([128, 4, 128])), ALU.mult)
                    P.tt("dve", yst[:], yst[:], zsb[r][:], ALU.mult)
                    for hh in range(4):
                        P.tr(psT[:, hh * 128:(hh + 1) * 128], yst[:, hh * 128:(hh + 1) * 128], ident[:],
                             partial=(hh > 0))
                    P.copy("act", View(aT3[:, hg * 4:(hg + 1) * 4, t0:t0 + 128], aT_b[hg * 4:(hg + 1) * 4]),
                           psT.v(psT.t[:].rearrange("p (h t) -> p h t", h=4)), partial=True)
                if si + 2 < len(steps):
                    load(si + 2)
            P.barrier()

    def odd_layer(l):
        j = l // 2
        norm_phase(lambda kc, kind: cf(0, kc, kind), lambda kc, kind: cf(1, kc, kind), TB)
        odd_inproj(j)
        if not cfg.get("skip_gdn", False):
            gdn_pass(j, 0)
            if not cfg.get("skip_gdn_b", False):
                gdn_pass(j, 1)
        if dbg:
            dd = nc.dram_tensor("dbg_aT%d" % l, [128, KD * NT], BF16, kind="ExternalOutput").ap()
            dbg_names.append("dbg_aT%d" % l)
            P.dma("sp", View(dd, []), View(aT.t[:], aT_b))
            P.barrier()
        out_proj(owout_d[j])

    for l in cfg.get("layer_list", list(range(nlayers))):
        modulation(l)
        if mixer_on:
            if l % 2 == 0:
                even_layer(l)
            elif cfg.get("odd", True):
                odd_layer(l)
        if ffn_on:
            norm_phase(lambda kc, kind: cf(3, kc, kind), lambda kc, kind: cf(4, kc, kind), TB)
            ffn_phase(l)

    fing = sb("fing", [128, KD])
    P.dma("sp", fing[:], View(fing_d, []))
    with ExitStack() as ph:
        sq = [ptile(ph, "fsq%d" % i, [128, 512]) for i in range(3)]
        rs = [ptile(ph, "frs%d" % i, [128, 512]) for i in range(2)]
        fo = [ptile(ph, "ffo%d" % i, [128, KD * 512]) for i in range(2)]
        stg = [ptile(ph, "sto%d" % i, [128, D]) for i in range(2)]
        n = 0
        nt_ = 0
        for bi, (t0, tn) in enumerate(TB[1:]):
            ps = PS[4 + bi % 2]
            for kc in range(KD):
                s = sq[n % 3]
                n += 1
                P.act(s[:, 0:tn], xTv(kc, t0, tn), AF.Square)
                P.mm(ps[:, 0:tn], onesD[:], s[:, 0:tn], start=(kc == 0), stop=(kc == KD - 1))
            r = rs[bi % 2]
            P.act(r[:, 0:tn], ps[:, 0:tn], AF.Sqrt, bias=EPS, scale=1.0)
            P.recip(r[:, 0:tn], r[:, 0:tn])
            f = fo[bi % 2]
            for kc in range(KD):
                P.stt("dve", f[:, kc * 512:kc * 512 + tn], xTv(kc, t0, tn), fing[:, kc:kc + 1], r[:, 0:tn],
                      ALU.mult, ALU.mult, partial=True)
            for q in range(tn // 128):
                ti = (t0 - NCTX) // 128 + q
                s_ = stg[nt_ % 2]
                for half in range(2):
                    pst = PS[(nt_ * 2 + half) % 4]
                    for qq in range(4):
                        kc = half * 4 + qq
                        P.tr(pst[:, qq * 128:(qq + 1) * 128], f[:, kc * 512 + q * 128:kc * 512 + (q + 1) * 128],
                             ident[:], partial=(qq > 0))
                    P.copy("act" if half == 0 else "dve", s_[:, half * 512:(half + 1) * 512], pst[:], partial=True)
                nt_ += 1
                P.dma("sp", View(out_d[ti * 128:(ti + 1) * 128, :], []), s_[:])
        P.barrier()
    es.close()
    return nc, P, dbg_names


def _blk_w(w, nblk):
    K = w.shape[0]
    return np.ascontiguousarray(w.reshape(K // 128, 128, nblk, 128).transpose(2, 1, 0, 3))


def _pm(v):
    return np.ascontiguousarray(v.reshape(-1, 128).T)


def prepare_inputs(inp):
    f = lambda a: np.asarray(a, dtype=np.float32)
    shared = {}
    shared["ada_w"] = np.stack([_blk_w(f(inp["ada_w"][l]), 48) for l in range(DEPTH)])
    shared["ada_b"] = np.stack([_pm(f(inp["ada_b"][l])) for l in range(DEPTH)])
    shared["norm1_g"] = np.stack([_pm(f(inp["norm1_g"][l])) for l in range(DEPTH)])
    shared["norm2_g"] = np.stack([_pm(f(inp["norm2_g"][l])) for l in range(DEPTH)])
    shared["final_g"] = _pm(f(inp["final_g"]))
    shared["f_w_up"] = np.stack([_blk_w(f(inp["f_w_up"][l]), 2 * NJ) for l in range(DEPTH)])
    fd = f(inp["f_dw"]).reshape(DEPTH, 9, NJ, 128).transpose(0, 3, 2, 1)
    shared["f_dw"] = np.ascontiguousarray(fd)
    shared["f_dw_b"] = np.stack([_pm(f(inp["f_dw_b"][l])) for l in range(DEPTH)])
    wd = f(inp["f_w_down"]).reshape(DEPTH, NJ, 128, KD, 128).transpose(0, 3, 2, 1, 4)
    shared["f_w_down"] = np.ascontiguousarray(wd)
    def wb(wc):
        return np.ascontiguousarray(wc.reshape(KD, 128, wc.shape[1]).transpose(1, 0, 2))

    def bc(v):
        return np.ascontiguousarray(np.broadcast_to(v[None, :], (128, v.shape[0])))

    ew = f(inp["e_w_in"])
    eb = f(inp["e_b_in"])
    fm_cols = [h * 128 for h in range(4)] + [512 + h * 128 for h in range(4)]
    for i in range(4):
        fm_cols += [2064 + i * 128, 2576 + i * 128]
    shared["e_w_fm"] = np.stack([np.stack([wb(ew[j][:, c:c + 128]) for c in fm_cols]) for j in range(2)])
    shared["e_b_fm"] = np.stack([np.stack([eb[j][c:c + 128] for c in fm_cols], axis=1) for j in range(2)])
    tm_cols = [512, 768, 1024, 1280, 1536, 1792]
    shared["e_w_tm"] = np.stack([np.stack([wb(ew[j][:, c:c + 256]) for c in tm_cols]) for j in range(2)])
    shared["e_w_g"] = np.stack([wb(ew[j][:, 2048:2064]) for j in range(2)])
    shared["e_b_tm"] = np.stack([bc(eb[j][512:2064]) for j in range(2)])
    shared["e_head_g_bc"] = np.stack([bc(f(inp["e_head_g"])[j]) for j in range(2)])
    shared["e_cdw"] = np.ascontiguousarray(f(inp["e_conf_dw"]).reshape(2, 31, 4, 128).transpose(0, 3, 2, 1))
    shared["e_cdwb"] = np.stack([_pm(f(inp["e_conf_dw_b"])[j]) for j in range(2)])
    shared["e_lng"] = np.stack([_pm(f(inp["e_conf_ln_g"])[j]) for j in range(2)])
    shared["e_lnb"] = np.stack([_pm(f(inp["e_conf_ln_b"])[j]) for j in range(2)])
    shared["e_w_out"] = np.stack([_blk_w(f(inp["e_w_out"])[j], KD) for j in range(2)])
    ow = f(inp["o_w_in"])
    shared["o_w_fm"] = np.stack([np.stack([wb(ow[j][:, b * 128:(b + 1) * 128]) for b in range(24)]) for j in range(2)])
    shared["o_w_tm"] = np.stack([np.stack([wb(ow[j][:, 3072 + g * 256:3072 + (g + 1) * 256]) for g in range(4)]) for j in range(2)])
    shared["o_w_g"] = np.stack([wb(ow[j][:, 4096:4128]) for j in range(2)])
    shared["o_short"] = np.ascontiguousarray(f(inp["o_short_w"]).reshape(2, 5, 24, 128).transpose(0, 3, 2, 1))
    shared["o_alog_bc"] = np.stack([bc(f(inp["o_a_log"])[j].reshape(16)) for j in range(2)])
    shared["o_dtb_bc"] = np.stack([bc(f(inp["o_dt_bias"])[j].reshape(16)) for j in range(2)])
    shared["o_head_g_bc"] = np.stack([bc(f(inp["o_head_g"])[j]) for j in range(2)])
    shared["o_w_out"] = np.stack([_blk_w(f(inp["o_w_out"])[j], KD) for j in range(2)])
    per_core = []
    x = f(inp["x"])
    ctx = f(inp["ctx"])
    c = f(inp["c"])
    cc = f(inp["c_ctx"])
    for b in range(8):
        d = dict(shared)
        d["xc"] = np.ascontiguousarray(np.concatenate([ctx[b], x[b]], axis=0))
        cv = np.stack([c[b], cc], axis=1)
        d["cvec"] = np.ascontiguousarray(cv.reshape(KD, 128, 2).transpose(1, 0, 2))
        per_core.append(d)
    return per_core


_CFG = {}


def kernel(**inputs):
    nc, P, _ = build_program(_CFG)
    in_maps = prepare_inputs(inputs)
    if not _CFG.get("ffn", True):
        in_maps = [{k: v for k, v in d.items() if not k.startswith("f_")} for d in in_maps]
    if not _CFG.get("mixer", True):
        in_maps = [{k: v for k, v in d.items() if not (k.startswith("e_") or k.startswith("o_"))} for d in in_maps]
    elif not _CFG.get("odd", True) or _CFG.get("layers", DEPTH) < 2:
        in_maps = [{k: v for k, v in d.items() if not k.startswith("o_")} for d in in_maps]
    res = run_bass_kernel_spmd(nc, in_maps, core_ids=list(range(8)))
    return np.stack([np.asarray(r["out"], dtype=np.float32) for r in res.results], axis=0)
```

```python
import numpy as np
from contextlib import ExitStack
import concourse.bass as bass
import concourse.mybir as mybir
from concourse.bass_utils import run_bass_kernel_spmd

F32 = mybir.dt.float32
BF16 = mybir.dt.bfloat16
AF = mybir.ActivationFunctionType
ALU = mybir.AluOpType
AX = mybir.AxisListType

NT = 2304
NTILE = 18
D = 1024
KD = 8
NCTX = 256
TB = [(0, 256), (256, 512), (768, 512), (1280, 512), (1792, 512)]
EPS = 1e-6
FFN = 2816
NJ = 22
BIG = 30000.0
DEPTH = 4


class Buf:
    __slots__ = ("w", "r", "prev")

    def __init__(self):
        self.w = {}
        self.r = {}
        self.prev = {}


class View:
    __slots__ = ("ap", "bufs", "psum")

    def __init__(self, ap, bufs, psum=False):
        self.ap = ap
        self.bufs = bufs
        self.psum = psum


class Tile:
    def __init__(self, t, psum=False):
        self.t = t
        self.buf = Buf()
        self.psum = psum

    def __getitem__(self, idx):
        return View(self.t[idx], [self.buf], self.psum)

    def v(self, ap):
        return View(ap, [self.buf], self.psum)


class Eng:
    def __init__(self, name, obj):
        self.name = name
        self.obj = obj
        self.cnt = 0
        self.csems = []
        self.seen = {}
        self.dma_n = 0
        self.rings = []


CEPOCH = 30000
RK = 8
REPOCH = 1800


class Prog:
    def __init__(self, nc, es):
        self.nc = nc
        self.es = es
        self.sems = []
        self.eng = {
            "pe": Eng("pe", nc.tensor),
            "act": Eng("act", nc.scalar),
            "dve": Eng("dve", nc.vector),
            "pool": Eng("pool", nc.gpsimd),
            "sp": Eng("sp", nc.sync),
        }
        self.ninst = 0
        self.nwait = 0

    def new_sem(self):
        s = self.es.enter_context(self.nc.semaphore("s%d" % len(self.sems)))
        self.sems.append(s)
        return len(self.sems) - 1

    def _ctag(self, e):
        ep = e.cnt // CEPOCH
        while len(e.csems) <= ep:
            e.csems.append(self.new_sem())
        e.cnt += 1
        return e.csems[ep], (e.cnt - 1) % CEPOCH + 1

    def _dtag(self, e, n):
        ep = n // (RK * REPOCH)
        while len(e.rings) <= ep:
            e.rings.append([self.new_sem() for _ in range(RK)])
        slot = n % RK
        return e.rings[ep][slot], 16 * ((n // RK) % REPOCH + 1)

    def op(self, en, fn, outs, ins, partial=False, dma=False):
        e = self.eng[en]
        deps = {}

        def add(tagd, skip_same):
            for sem, (val, ten, kind) in tagd.items():
                if skip_same and kind == "c" and ten == en:
                    continue
                if deps.get(sem, 0) < val:
                    deps[sem] = val

        for v in ins:
            for b in v.bufs:
                add(b.w, en == "pe")
                if v.psum:
                    add(b.r, True)
        for v in outs:
            for b in v.bufs:
                if partial and not b.r:
                    add(b.prev, True)
                    continue
                add(b.w, True)
                add(b.r, True)
        if dma and e.dma_n >= RK:
            ps, pv = self._dtag(e, e.dma_n - RK)
            if deps.get(ps, 0) < pv:
                deps[ps] = pv
        for sem, val in deps.items():
            if e.seen.get(sem, 0) >= val:
                continue
            e.obj.wait_ge(self.sems[sem], val)
            e.seen[sem] = val
            self.nwait += 1
        inst = fn()
        if dma:
            sem, val = self._dtag(e, e.dma_n)
            e.dma_n += 1
            inst.then_inc(self.sems[sem], 16)
            tag = (val, en, "d")
        else:
            sem, val = self._ctag(e)
            inst.then_inc(self.sems[sem], 1)
            tag = (val, en, "c")
        self.ninst += 1
        for v in ins:
            for b in v.bufs:
                if b.r.get(sem, (0,))[0] < val:
                    b.r[sem] = tag
        for v in outs:
            for b in v.bufs:
                if partial and not b.r:
                    b.w[sem] = tag
                else:
                    pv = dict(b.w)
                    for k_, t_ in b.r.items():
                        if pv.get(k_, (0,))[0] < t_[0]:
                            pv[k_] = t_
                    b.prev = pv
                    b.w = {sem: tag}
                    b.r = {}

    def barrier(self):
        tags = {}
        for e in self.eng.values():
            if e.cnt > 0:
                ep = (e.cnt - 1) // CEPOCH
                tags[e.csems[ep]] = (e.cnt - 1) % CEPOCH + 1
            for n in range(max(0, e.dma_n - RK), e.dma_n):
                s, v = self._dtag(e, n)
                if tags.get(s, 0) < v:
                    tags[s] = v
        for e in self.eng.values():
            for s, v in tags.items():
                if e.seen.get(s, 0) >= v:
                    continue
                e.obj.wait_ge(self.sems[s], v)
                e.seen[s] = v
                self.nwait += 1

    def mm(self, out, lhsT, rhs, start, stop):
        self.op("pe", lambda: self.nc.tensor.matmul(out.ap, lhsT.ap, rhs.ap, start=start, stop=stop),
                [out], [lhsT, rhs], partial=not start)

    def mm_part(self, out, lhsT, rhs, start, stop, partial):
        self.op("pe", lambda: self.nc.tensor.matmul(out.ap, lhsT.ap, rhs.ap, start=start, stop=stop),
                [out], [lhsT, rhs], partial=partial)

    def tr(self, out, in_, ident, partial=False):
        self.op("pe", lambda: self.nc.tensor.transpose(out=out.ap, in_=in_.ap, identity=ident.ap),
                [out], [in_, ident], partial=partial)

    def act(self, out, in_, func, bias=None, scale=None, partial=False, accum=None):
        ins = [in_]
        kw = {}
        if bias is not None:
            if isinstance(bias, View):
                ins.append(bias)
                kw["bias"] = bias.ap
            else:
                kw["bias"] = bias
        if scale is not None:
            if isinstance(scale, View):
                ins.append(scale)
                kw["scale"] = scale.ap
            else:
                kw["scale"] = scale
        outs = [out]
        if accum is not None:
            kw["accum_out"] = accum.ap
            outs.append(accum)
        self.op("act", lambda: self.nc.scalar.activation(out=out.ap, in_=in_.ap, func=func, **kw),
                outs, ins, partial=partial)

    def _veng(self, en):
        return self.nc.vector if en == "dve" else self.nc.gpsimd

    def tt(self, en, out, in0, in1, op, partial=False):
        o = self._veng(en)
        self.op(en, lambda: o.tensor_tensor(out=out.ap, in0=in0.ap, in1=in1.ap, op=op),
                [out], [in0, in1], partial=partial)

    def ts(self, en, out, in0, s1, s2, op0, op1=None, partial=False):
        o = self._veng(en)
        ins = [in0]
        a1 = s1
        a2 = s2
        if isinstance(s1, View):
            ins.append(s1)
            a1 = s1.ap
        if isinstance(s2, View):
            ins.append(s2)
            a2 = s2.ap
        if op1 is None:
            fn = lambda: o.tensor_scalar(out=out.ap, in0=in0.ap, scalar1=a1, scalar2=None, op0=op0)
        else:
            fn = lambda: o.tensor_scalar(out=out.ap, in0=in0.ap, scalar1=a1, scalar2=a2, op0=op0, op1=op1)
        self.op(en, fn, [out], ins, partial=partial)

    def stt(self, en, out, in0, scalar, in1, op0, op1, partial=False):
        en = "dve"
        o = self._veng(en)
        ins = [in0, in1]
        a = scalar
        if isinstance(scalar, View):
            ins.append(scalar)
            a = scalar.ap
        self.op(en, lambda: o.scalar_tensor_tensor(out=out.ap, in0=in0.ap, scalar=a, in1=in1.ap, op0=op0, op1=op1),
                [out], ins, partial=partial)

    def copy(self, en, out, in_, partial=False):
        if en == "act":
            self.act(out, in_, AF.Copy, partial=partial)
        else:
            o = self._veng(en)
            self.op(en, lambda: o.tensor_copy(out=out.ap, in_=in_.ap), [out], [in_], partial=partial)

    def memset(self, en, out, val):
        o = self._veng(en)
        self.op(en, lambda: o.memset(out.ap, val), [out], [])

    def recip(self, out, in_, partial=False):
        self.op("dve", lambda: self.nc.vector.reciprocal(out=out.ap, in_=in_.ap), [out], [in_], partial=partial)

    def dma(self, q, out, in_, partial=False, slow=False):
        o = self.nc.sync if q == "sp" else self.nc.gpsimd
        if slow:
            fn = lambda: o.dma_start(out=out.ap, in_=in_.ap, allow_slow_non_contiguous=True)
        else:
            fn = lambda: o.dma_start(out=out.ap, in_=in_.ap)
        self.op(q, fn, [out], [in_], partial=partial, dma=True)


def build_program(cfg):
    nc = bass.Bass("TRN2", target_bir_lowering=False)
    es = ExitStack()
    P = Prog(nc, es)
    dbg = cfg.get("debug", False)
    nlayers = cfg.get("layers", DEPTH)
    mixer_on = cfg.get("mixer", True)
    ffn_on = cfg.get("ffn", True)

    def din(name, shape):
        return nc.dram_tensor(name, list(shape), F32, kind="ExternalInput").ap()

    dbg_names = []

    def dscr(name, shape, dump=False, dt=F32):
        kind = "Internal"
        if dbg and dump:
            kind = "ExternalOutput"
            dbg_names.append(name)
        return nc.dram_tensor(name, list(shape), dt, kind=kind).ap()

    def sb(name, shape, dt=F32):
        return Tile(es.enter_context(nc.sbuf_tensor(name, list(shape), dt)))

    ucnt = [0]

    def ptile(ph, name, shape, dt=F32):
        ucnt[0] += 1
        return Tile(ph.enter_context(nc.sbuf_tensor("%s_u%d" % (name, ucnt[0]), list(shape), dt)))

    def load_w(dst, src_ap, stage, n, en="act", q="sp"):
        P.dma(q, stage[:, 0:n], View(src_ap, []))
        P.copy(en, dst[:, 0:n], stage[:, 0:n])

    xc_d = din("xc", [NT, D])
    cv_d = din("cvec", [128, KD, 2])
    adaw_d = din("ada_w", [DEPTH, 48, 128, KD, 128])
    adab_d = din("ada_b", [DEPTH, 128, 48])
    n1g_d = din("norm1_g", [DEPTH, 128, KD])
    n2g_d = din("norm2_g", [DEPTH, 128, KD])
    fing_d = din("final_g", [128, KD])
    if ffn_on:
        fwup_d = din("f_w_up", [DEPTH, 2 * NJ, 128, KD, 128])
        fdw_d = din("f_dw", [DEPTH, 128, NJ, 9])
        fdwb_d = din("f_dw_b", [DEPTH, 128, NJ])
        fwdn_d = din("f_w_down", [DEPTH, KD, 128, NJ, 128])
    out_d = nc.dram_tensor("out", [2048, D], F32, kind="ExternalOutput").ap()
    act_scr = dscr("act_scr", [NJ, 128, NT], dt=BF16) if ffn_on else None
    act_scr_b = [[Buf() for _ in TB] for _ in range(NJ)]

    xT = sb("xT", [128, KD * NT])
    aT = sb("aT", [128, KD * NT], BF16)
    xT3 = xT.t[:].rearrange("p (k t) -> p k t", k=KD)
    aT3 = aT.t[:].rearrange("p (k t) -> p k t", k=KD)
    ident = sb("ident", [128, 128])
    onesD = sb("onesD", [128, 128])
    modT = sb("modT", [128, 96])
    PS = [Tile(es.enter_context(nc.psum_tensor("ps%d" % i, [128, 512], F32)), psum=True) for i in range(8)]

    xT_b = [Buf() for _ in range(KD)]
    aT_b = [Buf() for _ in range(KD)]

    def xTv(kc, t0, n):
        return View(xT3[:, kc, t0:t0 + n], [xT_b[kc]])

    def aTv(kc, t0, n):
        return View(aT3[:, kc, t0:t0 + n], [aT_b[kc]])

    P.memset("pool", ident[:], 1.0)
    P.op("pool", lambda: nc.gpsimd.affine_select(out=ident.t[:], in_=ident.t[:], pattern=[[-1, 128]],
                                                  compare_op=ALU.is_equal, fill=0.0, base=0, channel_multiplier=1),
         [ident[:]], [ident[:]])
    P.memset("pool", onesD[:], 1.0 / D)

    with ExitStack() as ph:
        stg = [ptile(ph, "ldx%d" % i, [128, D]) for i in range(2)]
        for ti in range(NTILE):
            s = stg[ti % 2]
            P.dma("sp", s[:], View(xc_d[ti * 128:(ti + 1) * 128, :], []))
            for half in range(2):
                ps = PS[(ti * 2 + half) % 4]
                for q in range(4):
                    kc = half * 4 + q
                    P.tr(ps[:, q * 128:(q + 1) * 128], s[:, kc * 128:(kc + 1) * 128], ident[:], partial=(q > 0))
                dst = View(xT3[:, half * 4:(half + 1) * 4, ti * 128:(ti + 1) * 128], xT_b[half * 4:(half + 1) * 4])
                src = ps.v(ps.t[:].rearrange("p (q t) -> p q t", q=4))
                P.copy("act" if half == 0 else "dve", dst, src, partial=True)
        P.barrier()

    scv = sb("scv", [128, KD * 2])
    P.dma("sp", scv[:], View(cv_d.rearrange("p k j -> p (k j)"), []))
    P.act(scv[:], scv[:], AF.Silu)
    scv3 = scv.t[:].rearrange("p (k j) -> p k j", j=2)

    coef = sb("coef", [128, 6 * KD * 2])
    coef4 = coef.t[:].rearrange("p (g k j) -> p g k j", g=6, k=KD)
    nrm_g = sb("nrm_g", [128, 2 * KD])
    adab = sb("adab", [128, 48])
    ones_c = sb("ones_c", [128, 2 * KD])
    P.memset("pool", ones_c[:], 1.0)

    def modulation(l):
        with ExitStack() as ph:
            wb = [ptile(ph, "adw%d" % i, [128, KD * 128]) for i in range(3)]
            P.dma("pool", adab[:], View(adab_d[l], []))
            P.dma("pool", nrm_g[:, 0:KD], View(n1g_d[l], []))
            P.dma("pool", nrm_g[:, KD:2 * KD], View(n2g_d[l], []))
            ps = PS[7]
            for j in range(48):
                w = wb[j % 3]
                P.dma("sp", w[:], View(adaw_d[l, j].rearrange("p k f -> p (k f)"), []))
                for kc in range(KD):
                    P.mm_part(ps[:, 2 * j:2 * j + 2], w[:, kc * 128:(kc + 1) * 128], scv.v(scv3[:, kc, :]),
                              start=(kc == 0), stop=(kc == KD - 1), partial=not (j == 0 and kc == 0))
            m3 = modT.t[:].rearrange("p (j k) -> p j k", k=2)
            P.tt("dve", modT.v(m3), ps.v(ps.t[:, 0:96].rearrange("p (j k) -> p j k", k=2)),
                 adab.v(adab.t[:].unsqueeze(2).to_broadcast([128, 48, 2])), ALU.add)
            def grp(g):
                return modT.v(m3[:, g * 8:(g + 1) * 8, :])
            for half, (gsc, gsh, gg) in enumerate([(1, 0, 2), (4, 3, 5)]):
                gam = nrm_g.v(nrm_g.t[:, half * KD:(half + 1) * KD].unsqueeze(2).to_broadcast([128, KD, 2]))
                A = coef.v(coef4[:, half * 3 + 0])
                P.stt("dve", A, grp(gsc), 1.0, gam, ALU.add, ALU.mult, partial=True)
                P.copy("dve", coef.v(coef4[:, half * 3 + 1]), grp(gsh), partial=True)
                P.copy("dve", coef.v(coef4[:, half * 3 + 2]), grp(gg), partial=True)
            P.barrier()

    def cf(g, kc, kind):
        return coef.v(coef4[:, g, kc, kind:kind + 1])

    def norm_phase(Afn, Bfn, tbs):
        with ExitStack() as ph:
            sq = [ptile(ph, "sq%d" % i, [128, 512]) for i in range(3)]
            rs = [ptile(ph, "rs%d" % i, [128, 512]) for i in range(2)]
            tm = [ptile(ph, "tm%d" % i, [128, 512]) for i in range(3)]
            n = 0
            for bi, (t0, tn) in enumerate(tbs):
                kind = 1 if t0 < NCTX else 0
                ps = PS[4 + bi % 2]
                for kc in range(KD):
                    s = sq[n % 3]
                    n += 1
                    P.act(s[:, 0:tn], xTv(kc, t0, tn), AF.Square)
                    P.mm(ps[:, 0:tn], onesD[:], s[:, 0:tn], start=(kc == 0), stop=(kc == KD - 1))
                r = rs[bi % 2]
                P.act(r[:, 0:tn], ps[:, 0:tn], AF.Sqrt, bias=EPS, scale=1.0)
                P.recip(r[:, 0:tn], r[:, 0:tn])
                for kc in range(KD):
                    t = tm[kc % 3]
                    P.tt("dve", t[:, 0:tn], xTv(kc, t0, tn), r[:, 0:tn], ALU.mult)
                    P.act(aTv(kc, t0, tn), t[:, 0:tn], AF.Identity, bias=Bfn(kc, kind), scale=Afn(kc, kind),
                          partial=True)
            P.barrier()

    fdw = sb("fdw", [128, NJ * 9])
    fdwb = sb("fdwb", [128, NJ])
    fdw3 = fdw.t[:].rearrange("p (j k) -> p j k", k=9)

    def ffn_phase(l):
        P.dma("pool", fdw[:], View(fdw_d[l].rearrange("p j k -> p (j k)"), []))
        P.dma("pool", fdwb[:], View(fdwb_d[l], []))
        with ExitStack() as ph:
            wgs = [ptile(ph, "wgs%d" % i, [128, KD * 128]) for i in range(2)]
            wvs = [ptile(ph, "wvs%d" % i, [128, KD * 128]) for i in range(2)]
            wg = [ptile(ph, "wg%d" % i, [128, KD * 128], BF16) for i in range(2)]
            wv = [ptile(ph, "wv%d" % i, [128, KD * 128], BF16) for i in range(2)]
            G = [ptile(ph, "G%d" % i, [128, NT]) for i in range(1)]
            Gc = [ptile(ph, "Gc%d" % i, [128, NT]) for i in range(2)]
            gcp = [ptile(ph, "gcp%d" % i, [128, NT - NCTX]) for i in range(1)]
            ptmp = ptile(ph, "ptmp", [128, NT - NCTX])
            Vv = [ptile(ph, "Vv%d" % i, [128, NT]) for i in range(2)]
            gh = [ptile(ph, "gh%d" % i, [128, NT], BF16) for i in range(1)]
            npb = 0
            for j in range(NJ):
                g_w = wg[j % 2]
                v_w = wv[j % 2]
                load_w(g_w, fwup_d[l, j].rearrange("p k f -> p (k f)"), wgs[j % 2], KD * 128)
                load_w(v_w, fwup_d[l, NJ + j].rearrange("p k f -> p (k f)"), wvs[j % 2], KD * 128)
                g = G[0]
                gc = Gc[j % 2]
                vv = Vv[j % 2]
                for bi, (t0, tn) in enumerate(TB):
                    for which, (w, dst, en) in enumerate([(g_w, g, "act"), (v_w, vv, "act")]):
                        ps = PS[npb % 8]
                        npb += 1
                        for kc in range(KD):
                            P.mm(ps[:, 0:tn], w[:, kc * 128:(kc + 1) * 128], aTv(kc, t0, tn),
                                 start=(kc == 0), stop=(kc == KD - 1))
                        P.copy(en, dst[:, t0:t0 + tn], ps[:, 0:tn], partial=True)
                def wk(i, jj):
                    return fdw.v(fdw3[:, j, i * 3 + jj:i * 3 + jj + 1])
                P.ts("dve", gc[:, 0:NCTX], g[:, 0:NCTX], wk(1, 1), None, ALU.mult, partial=True)
                P.stt("pool", gc[:, 1:NCTX], g[:, 0:NCTX - 1], wk(1, 0), gc[:, 1:NCTX], ALU.mult, ALU.add, partial=True)
                P.stt("pool", gc[:, 0:NCTX - 1], g[:, 1:NCTX], wk(1, 2), gc[:, 0:NCTX - 1], ALU.mult, ALU.add, partial=True)
                g3 = g.t[:, NCTX:NT].rearrange("p (r w) -> p r w", w=64)
                c3 = gc.t[:, NCTX:NT].rearrange("p (r w) -> p r w", w=64)
                P.ts("dve", gc[:, NCTX:NT], g[:, NCTX:NT], wk(1, 1), None, ALU.mult, partial=True)
                k = 0
                gp = gcp[0]
                p3 = gp.t[:].rearrange("p (r w) -> p r w", w=64)
                t3 = ptmp.t[:].rearrange("p (r w) -> p r w", w=64)
                for i in range(3):
                    for jj in range(3):
                        if i == 1 and jj == 1:
                            continue
                        dr, dc = i - 1, jj - 1
                        r0, r1 = max(0, -dr), min(32, 32 - dr)
                        c0, c1 = max(0, -dc), min(64, 64 - dc)
                        src = g.v(g3[:, r0 + dr:r1 + dr, c0 + dc:c1 + dc])
                        if False:
                            P.ts("pool", ptmp.v(t3[:, r0:r1, c0:c1]), src, wk(i, jj), None, ALU.mult, partial=True)
                            P.tt("pool", gp.v(p3[:, r0:r1, c0:c1]), gp.v(p3[:, r0:r1, c0:c1]),
                                 ptmp.v(t3[:, r0:r1, c0:c1]), ALU.add, partial=True)
                        else:
                            P.stt("dve", gc.v(c3[:, r0:r1, c0:c1]), src,
                                  wk(i, jj), gc.v(c3[:, r0:r1, c0:c1]), ALU.mult, ALU.add, partial=True)
                        k += 1
                P.act(gc[:], gc[:], AF.Silu, bias=fdwb[:, j:j + 1], scale=1.0)
                P.tt("dve", gh[0][:], gc[:], vv[:], ALU.mult)
                for bi, (t0, tn) in enumerate(TB):
                    P.dma("pool", View(act_scr[j, :, t0:t0 + tn], [act_scr_b[j][bi]]), gh[0][:, t0:t0 + tn])
            P.barrier()
        with ExitStack() as ph:
            wds = [ptile(ph, "wds%d" % i, [128, NJ * 128]) for i in range(2)]
            wd = [ptile(ph, "wd%d" % i, [128, NJ * 128], BF16) for i in range(2)]
            abk = [ptile(ph, "abk%d" % i, [128, NJ * 512], BF16) for i in range(2)]
            nb = 0
            nw = 0
            for bi, (t0, tn) in enumerate(TB):
                kind = 1 if t0 < NCTX else 0
                ab_t = abk[bi % 2]
                region = ab_t.t[:].rearrange("p (j t) -> p j t", j=NJ)
                for j in range(NJ):
                    P.dma("sp", ab_t.v(region[:, j, 0:tn]), View(act_scr[j, :, t0:t0 + tn], [act_scr_b[j][bi]]),
                          partial=(j > 0))
                for i in range(KD):
                    w = wd[nw % 2]
                    load_w(w, fwdn_d[l, i].rearrange("p j f -> p (j f)"), wds[nw % 2], NJ * 128)
                    nw += 1
                    ps = PS[nb % 8]
                    nb += 1
                    for j in range(NJ):
                        P.mm(ps[:, 0:tn], w[:, j * 128:(j + 1) * 128], ab_t.v(region[:, j, 0:tn]),
                             start=(j == 0), stop=(j == NJ - 1))
                    P.stt("dve", xTv(i, t0, tn), ps[:, 0:tn], cf(5, i, kind), xTv(i, t0, tn), ALU.mult, ALU.add,
                          partial=True)
            P.barrier()

    triF = sb("triF", [128, 128])
    triB = sb("triB", [128, 128])
    ones128 = sb("ones128", [128, 128])
    negiF = sb("negiF", [128, 128])
    negiB = sb("negiB", [128, 128])
    possF = sb("possF", [128, 128])
    possB = sb("possB", [128, 128])
    P.memset("pool", ones128[:], 1.0)

    def aff(tile, val, step, cm, op, fill):
        P.memset("pool", tile[:], val)
        P.op("pool", lambda: nc.gpsimd.affine_select(out=tile.t[:], in_=tile.t[:], pattern=[[step, 128]],
                                                      compare_op=op, fill=fill, base=0, channel_multiplier=cm),
             [tile[:]], [tile[:]])

    aff(triF, 1.0, 1, -1, ALU.is_ge, 0.0)
    aff(triB, 1.0, -1, 1, ALU.is_ge, 0.0)
    aff(negiF, 0.0, 1, -1, ALU.is_ge, -BIG)
    aff(negiB, 0.0, -1, 1, ALU.is_ge, -BIG)
    aff(possF, 0.0, -1, 1, ALU.is_gt, BIG)
    aff(possB, 0.0, 1, -1, ALU.is_gt, BIG)

    if mixer_on:
        qT_s = dscr("qT_s", dump=True, shape=[1024, NT])
        kT_s = dscr("kT_s", dump=True, shape=[1024, NT])
        ktok_s = dscr("ktok_s", dump=True, shape=[NT, 1024])
        vtok_s = dscr("vtok_s", dump=True, shape=[NT, 1024])
        z_s = dscr("z_s", dump=True, shape=[NT, 1024])
        gates_s = dscr("gates_s", dump=True, shape=[NT, 32])
        yf_s = dscr("yf_s", dump=True, shape=[NT, 1024])
        ucv_s = dscr("ucv_s", dump=True, shape=[4, 128, NT])
    llist = cfg.get("layer_list", list(range(nlayers)))
    if mixer_on and any(l_ % 2 == 0 for l_ in llist):
        ewfm_d = din("e_w_fm", [2, 16, 128, KD, 128])
        ewtm_d = din("e_w_tm", [2, 6, 128, KD, 256])
        ewg_d = din("e_w_g", [2, 128, KD, 16])
        ebfm_d = din("e_b_fm", [2, 128, 16])
        ebtm_d = din("e_b_tm", [2, 128, 1552])
        ehg_d = din("e_head_g_bc", [2, 128, 512])
        ecdw_d = din("e_cdw", [2, 128, 4, 31])
        ecdwb_d = din("e_cdwb", [2, 128, 4])
        elng_d = din("e_lng", [2, 128, 4])
        elnb_d = din("e_lnb", [2, 128, 4])
        ewout_d = din("e_w_out", [2, KD, 128, KD, 128])

    def out_proj(w_d):
        with ExitStack() as ph:
            wbs = [ptile(ph, "wos%d" % i, [128, KD * 128]) for i in range(2)]
            wb = [ptile(ph, "wo%d" % i, [128, KD * 128], BF16) for i in range(2)]
            n = 0
            for i in range(KD):
                w = wb[i % 2]
                load_w(w, w_d[i].rearrange("p k f -> p (k f)"), wbs[i % 2], KD * 128)
                for bi, (t0, tn) in enumerate(TB):
                    kind = 1 if t0 < NCTX else 0
                    ps = PS[n % 8]
                    n += 1
                    for kc in range(KD):
                        P.mm(ps[:, 0:tn], w[:, kc * 128:(kc + 1) * 128], aTv(kc, t0, tn),
                             start=(kc == 0), stop=(kc == KD - 1))
                    P.stt("dve", xTv(i, t0, tn), ps[:, 0:tn], cf(2, i, kind), xTv(i, t0, tn), ALU.mult, ALU.add,
                          partial=True)
            P.barrier()

    def even_inproj(j):
        SC = 128.0 ** -0.5
        with ExitStack() as ph:
            ebfm = ptile(ph, "ebfm", [128, 16])
            P.dma("pool", ebfm[:], View(ebfm_d[j], []))
            ebq = ptile(ph, "ebq", [128, 4])
            P.ts("dve", ebq[:], ebfm[:, 0:4], SC, None, ALU.mult)
            cdw = ptile(ph, "cdw", [128, 4 * 31])
            cdwb = ptile(ph, "cdwb", [128, 4])
            P.dma("pool", cdw[:], View(ecdw_d[j].rearrange("p i k -> p (i k)"), []))
            P.dma("pool", cdwb[:], View(ecdwb_d[j], []))
            with ExitStack() as ph2:
                wbs = [ptile(ph2, "ewfs%d" % i, [128, KD * 128]) for i in range(2)]
                wb = [ptile(ph2, "ewf%d" % i, [128, KD * 128], BF16) for i in range(2)]
                st = [ptile(ph2, "est%d" % i, [128, NT]) for i in range(2)]
                uu = ptile(ph2, "euu", [128, NT])
                cv = ptile(ph2, "ecv", [128, NT])
                npb = 0
                for blk in range(16):
                    w = wb[blk % 2]
                    load_w(w, ewfm_d[j, blk].rearrange("p k f -> p (k f)"), wbs[blk % 2], KD * 128)
                    s = st[blk % 2]
                    for bi, (t0, tn) in enumerate(TB):
                        ps = PS[npb % 4]
                        npb += 1
                        for kc in range(KD):
                            P.mm(ps[:, 0:tn], w[:, kc * 128:(kc + 1) * 128], aTv(kc, t0, tn),
                                 start=(kc == 0), stop=(kc == KD - 1))
                        if blk < 4:
                            P.act(s[:, t0:t0 + tn], ps[:, 0:tn], AF.Identity, bias=ebq[:, blk:blk + 1], scale=SC,
                                  partial=True)
                        elif blk < 8 or blk % 2 == 0:
                            P.act(s[:, t0:t0 + tn], ps[:, 0:tn], AF.Identity, bias=ebfm[:, blk:blk + 1], scale=1.0,
                                  partial=True)
                        else:
                            P.act(s[:, t0:t0 + tn], ps[:, 0:tn], AF.Sigmoid, bias=ebfm[:, blk:blk + 1], scale=1.0,
                                  partial=True)
                    if blk < 4:
                        P.dma("pool", View(qT_s[blk * 128:(blk + 1) * 128, :], []), s[:])
                    elif blk < 8:
                        P.dma("pool", View(kT_s[(blk - 4) * 128:(blk - 3) * 128, :], []), s[:])
                    elif blk % 2 == 1:
                        i = (blk - 8) // 2
                        a_st = st[(blk - 1) % 2]
                        P.tt("dve", uu[:], a_st[:], s[:], ALU.mult)

                        def wk(k):
                            return cdw[:, i * 31 + k:i * 31 + k + 1]
                        P.ts("dve", cv[:], uu[:], wk(15), None, ALU.mult)
                        for k in range(31):
                            if k == 15:
                                continue
                            off = k - 15
                            for (s0, s1) in ((0, NCTX), (NCTX, NT)):
                                a = max(s0, s0 - off)
                                b = min(s1, s1 - off)
                                if b <= a:
                                    continue
                                P.stt("dve", cv[:, a:b], uu[:, a + off:b + off], wk(k), cv[:, a:b], ALU.mult, ALU.add,
                                      partial=True)
                        P.ts("dve", cv[:], cv[:], cdwb[:, i:i + 1], None, ALU.add)
                        P.dma("pool", View(ucv_s[i], []), cv[:])
                P.barrier()
            with ExitStack() as ph2:
                wts = [ptile(ph2, "ewts%d" % i, [128, KD * 256]) for i in range(2)]
                wt = [ptile(ph2, "ewt%d" % i, [128, KD * 256], BF16) for i in range(2)]
                wgs_ = ptile(ph2, "ewgs", [128, KD * 16])
                wg_ = ptile(ph2, "ewg", [128, KD * 16], BF16)
                ebtm = ptile(ph2, "ebtm", [128, 1552])
                P.dma("pool", ebtm[:], View(ebtm_d[j], []))
                stg = [ptile(ph2, "etm%d" % i, [128, 256]) for i in range(3)]
                gst = [ptile(ph2, "egs%d" % i, [128, 16]) for i in range(2)]
                gtmp = [ptile(ph2, "egt%d" % i, [128, 4]) for i in range(2)]
                n = 0
                for g in range(6):
                    w = wt[g % 2]
                    load_w(w, ewtm_d[j, g].rearrange("p k f -> p (k f)"), wts[g % 2], KD * 256)
                    dst = (ktok_s, vtok_s, z_s)[g // 2]
                    c0 = (g % 2) * 256
                    for ti in range(NTILE):
                        ps = PS[n % 4]
                        s = stg[n % 3]
                        n += 1
                        for kc in range(KD):
                            P.mm(ps[:, 0:256], aTv(kc, ti * 128, 128), w[:, kc * 256:(kc + 1) * 256],
                                 start=(kc == 0), stop=(kc == KD - 1))
                        P.tt("dve", s[:], ps[:, 0:256], ebtm[:, g * 256:(g + 1) * 256], ALU.add)
                        if g >= 4:
                            P.act(s[:], s[:], AF.Sigmoid)
                        P.dma("pool", View(dst[ti * 128:(ti + 1) * 128, c0:c0 + 256], []), s[:])
                load_w(wg_, ewg_d[j].rearrange("p k f -> p (k f)"), wgs_, KD * 16)
                for ti in range(NTILE):
                    ps = PS[4 + ti % 2]
                    s = gst[ti % 2]
                    tmp = gtmp[ti % 2]
                    for kc in range(KD):
                        P.mm(ps[:, 0:16], aTv(kc, ti * 128, 128), wg_[:, kc * 16:(kc + 1) * 16],
                             start=(kc == 0), stop=(kc == KD - 1))
                    P.tt("dve", s[:], ps[:, 0:16], ebtm[:, 1536:1552], ALU.add)
                    for c0 in (4, 12):
                        P.act(tmp[:], s[:, c0:c0 + 4], AF.Exp, scale=-1.0)
                        P.act(tmp[:], tmp[:], AF.Ln, bias=1.0, scale=1.0)
                        P.ts("dve", s[:, c0:c0 + 4], tmp[:], -1.0, None, ALU.mult, partial=True)
                    P.dma("pool", View(gates_s[ti * 128:(ti + 1) * 128, 0:16], []), s[:])
            P.barrier()

    def conformer_ln(j):
        with ExitStack() as ph:
            lng = ptile(ph, "lng", [128, 4])
            lnb = ptile(ph, "lnb", [128, 4])
            P.dma("pool", lng[:], View(elng_d[j], []))
            P.dma("pool", lnb[:], View(elnb_d[j], []))
            onesC = ptile(ph, "onesC", [128, 128])
            P.memset("pool", onesC[:], 1.0 / 512)
            cw = [ptile(ph, "cw%d" % i, [128, NT]) for i in range(4)]
            for i in range(4):
                P.dma("sp", cw[i][:], View(ucv_s[i], []))
            sq = [ptile(ph, "csq%d" % i, [128, 512]) for i in range(2)]
            mt = [ptile(ph, "cmt%d" % i, [128, 512]) for i in range(2)]
            rt = [ptile(ph, "crt%d" % i, [128, 512]) for i in range(2)]
            for bi, (t0, tn) in enumerate(TB):
                psm = PS[4 + bi % 2]
                for i in range(4):
                    P.mm(psm[:, 0:tn], onesC[:], cw[i][:, t0:t0 + tn], start=(i == 0), stop=(i == 3))
                mean = mt[bi % 2]
                P.copy("act", mean[:, 0:tn], psm[:, 0:tn])
                for i in range(4):
                    P.tt("dve", cw[i][:, t0:t0 + tn], cw[i][:, t0:t0 + tn], mean[:, 0:tn], ALU.subtract, partial=True)
                psv = PS[6 + bi % 2]
                for i in range(4):
                    s = sq[i % 2]
                    P.act(s[:, 0:tn], cw[i][:, t0:t0 + tn], AF.Square)
                    P.mm(psv[:, 0:tn], onesC[:], s[:, 0:tn], start=(i == 0), stop=(i == 3))
                r = rt[bi % 2]
                P.act(r[:, 0:tn], psv[:, 0:tn], AF.Sqrt, bias=EPS, scale=1.0)
                P.recip(r[:, 0:tn], r[:, 0:tn])
                for i in range(4):
                    P.tt("dve", cw[i][:, t0:t0 + tn], cw[i][:, t0:t0 + tn], r[:, 0:tn], ALU.mult, partial=True)
                    P.act(aTv(4 + i, t0, tn), cw[i][:, t0:t0 + tn], AF.Silu, bias=lnb[:, i:i + 1],
                          scale=lng[:, i:i + 1], partial=True)
            P.barrier()

    def mlstm_pass(j, direction):
        fwd = direction == 0
        tri = triF if fwd else triB
        negi = negiF if fwd else negiB
        base = 0 if fwd else 8
        order = list(range(NTILE)) if fwd else [1, 0] + list(range(NTILE - 1, 1, -1))
        with ExitStack() as ph:
            Cx = [ptile(ph, "Cx%d" % h, [128, 129]) for h in range(4)]
            for h in range(4):
                P.memset("pool", Cx[h][:], 0.0)
            qTb = [ptile(ph, "mq%d" % i, [128, 512]) for i in range(2)]
            kTb = [ptile(ph, "mk%d" % i, [128, 512]) for i in range(2)]
            ktb = [ptile(ph, "mkt%d" % i, [128, 512]) for i in range(2)]
            vxb = [ptile(ph, "mvx%d" % i, [128, 4 * 129]) for i in range(2)]
            gtb = [ptile(ph, "mg%d" % i, [128, 16]) for i in range(2)]
            for vx in vxb:
                P.memset("pool", vx[:], 1.0)
            if not fwd:
                yfb = [ptile(ph, "myf%d" % i, [128, 512]) for i in range(2)]
                ogb = [ptile(ph, "mog%d" % i, [128, 512]) for i in range(2)]
                hgbc = ptile(ph, "hgbc", [128, 512])
                P.dma("pool", hgbc[:], View(ehg_d[j], []))
                ysq = ptile(ph, "ysq", [128, 512])
                gg = ptile(ph, "gg", [128, 512])
                ssb = [ptile(ph, "ss%d" % i, [128, 4]) for i in range(2)]
            G2 = [ptile(ph, "G2%d" % i, [128, 512]) for i in range(2)]
            eBB = [ptile(ph, "eBB%d" % i, [128, 512]) for i in range(2)]
            Bcolb = [ptile(ph, "Bc%d" % i, [128, 4]) for i in range(2)]
            b1b = [ptile(ph, "b1%d" % i, [128, 4]) for i in range(2)]
            wlb = [ptile(ph, "wl%d" % i, [128, 4]) for i in range(2)]
            eBlb = [ptile(ph, "eBl%d" % i, [128, 4]) for i in range(2)]
            tmpTb = [ptile(ph, "tT%d" % i, [128, 128]) for i in range(2)]
            DmTb = [ptile(ph, "DmT%d" % i, [128, 128]) for i in range(2)]
            PTb = [ptile(ph, "PT%d" % i, [128, 128]) for i in range(2)]
            qbTb = [ptile(ph, "qbT%d" % i, [128, 128]) for i in range(2)]
            kwb = [ptile(ph, "kw%d" % i, [128, 128]) for i in range(2)]
            ystb = [ptile(ph, "yst%d" % i, [128, 512]) for i in range(2)]
            rdb = [ptile(ph, "rd%d" % i, [128, 1]) for i in range(2)]
            psBB, psCol, psCol2 = PS[0], PS[1], PS[7]
            psST = [PS[2], PS[3]]
            psNum = [PS[4], PS[5]]
            psC = PS[6]

            def load(ci):
                r = ci % 2
                c = order[ci]
                t0 = c * 128
                P.dma("sp", qTb[r].v(qTb[r].t[:].rearrange("p (h t) -> p h t", h=4)),
                      View(qT_s[0:512, t0:t0 + 128].rearrange("(h p) t -> p h t", p=128), []))
                P.dma("sp", kTb[r].v(kTb[r].t[:].rearrange("p (h t) -> p h t", h=4)),
                      View(kT_s[0:512, t0:t0 + 128].rearrange("(h p) t -> p h t", p=128), []))
                P.dma("sp", ktb[r][:], View(ktok_s[t0:t0 + 128, 0:512], []))
                P.dma("sp", vxb[r].v(vxb[r].t[:].rearrange("p (h d) -> p h d", h=4)[:, :, 0:128]),
                      View(vtok_s[t0:t0 + 128, 0:512].rearrange("p (h d) -> p h d", h=4), []), partial=True)
                P.dma("sp", gtb[r][:], View(gates_s[t0:t0 + 128, 0:16], []))
                if not fwd:
                    P.dma("sp", yfb[r][:], View(yf_s[t0:t0 + 128, 0:512], []))
                    P.dma("sp", ogb[r][:], View(z_s[t0:t0 + 128, 0:512], []))

            load(0)
            nu = 0
            for ci, c in enumerate(order):
                r = ci % 2
                t0 = c * 128
                if ci + 1 < len(order):
                    load(ci + 1)
                qTr, kTr, ktr, vx, gt = qTb[r], kTb[r], ktb[r], vxb[r], gtb[r]
                G2r, eBBr, Bcol, b1, wl, eBl = G2[r], eBB[r], Bcolb[r], b1b[r], wlb[r], eBlb[r]
                lf = gt[:, base + 4:base + 8]
                li = gt[:, base:base + 4]
                P.tt("dve", G2r.v(G2r.t[:].rearrange("p (h s) -> p h s", h=4)),
                     tri.v(tri.t[:].unsqueeze(1).to_broadcast([128, 4, 128])),
                     gt.v(gt.t[:, base + 4:base + 8].unsqueeze(2).to_broadcast([128, 4, 128])), ALU.mult)
                P.mm(psBB[:, 0:512], ones128[:], G2r[:], True, True)
                P.mm(psCol[:, 0:4], tri[:], lf, True, True)
                P.mm(psCol2[:, 0:4], ones128[:], lf, True, True)
                P.copy("act", Bcol[:], psCol[:, 0:4])
                P.tt("dve", b1[:], li, psCol[:, 0:4], ALU.subtract)
                P.tt("dve", wl[:], b1[:], psCol2[:, 0:4], ALU.add)
                P.act(wl[:], wl[:], AF.Exp)
                P.act(eBl[:], psCol2[:, 0:4], AF.Exp)
                P.act(eBBr[:], psBB[:, 0:512], AF.Exp)
                yst = ystb[r]
                for h in range(4):
                    u = nu % 2
                    nu += 1
                    hs = slice(h * 128, (h + 1) * 128)
                    psS = psST[u]
                    psN = psNum[u]
                    tmpT, DmT, PT, qbT, kw, rden = tmpTb[u], DmTb[u], PTb[u], qbTb[u], kwb[u], rdb[u]
                    P.mm(psS[:, 0:128], kTr[:, hs], qTr[:, hs], True, True)
                    P.stt("dve", tmpT[:], psBB[:, hs], Bcol[:, h:h + 1], negi[:], ALU.min, ALU.add)
                    P.act(DmT[:], tmpT[:], AF.Exp, bias=b1[:, h:h + 1], scale=1.0)
                    P.tt("dve", PT[:], psS[:, 0:128], DmT[:], ALU.mult)
                    P.tt("dve", qbT[:], qTr[:, hs], eBBr[:, hs], ALU.mult)
                    P.mm(psN[:, 0:129], qbT[:], Cx[h][:], True, False)
                    P.mm(psN[:, 0:129], PT[:], vx[:, h * 129:(h + 1) * 129], False, True)
                    P.ts("dve", rden[:], psN[:, 128:129], -1.0, None, ALU.mult)
                    P.stt("dve", rden[:], psN[:, 128:129], 1.0, rden[:], ALU.max, ALU.max)
                    P.recip(rden[:], rden[:])
                    if fwd:
                        P.act(yst[:, hs], psN[:, 0:128], AF.Identity, bias=0.0, scale=rden[:, 0:1], partial=True)
                    else:
                        P.stt("dve", yst[:, hs], psN[:, 0:128], rden[:, 0:1], yfb[r][:, hs], ALU.mult, ALU.add,
                              partial=True)
                    P.act(kw[:], ktr[:, hs], AF.Identity, bias=0.0, scale=wl[:, h:h + 1])
                    P.mm(psC[:, 0:129], kw[:], vx[:, h * 129:(h + 1) * 129], True, True)
                    P.stt("dve", Cx[h][:], Cx[h][:], eBl[:, h:h + 1], psC[:, 0:129], ALU.mult, ALU.add)
                if fwd:
                    P.dma("pool", View(yf_s[t0:t0 + 128, 0:512], []), yst[:])
                else:
                    ss = ssb[r]
                    P.tt("dve", ysq[:], yst[:], yst[:], ALU.mult)
                    P.op("dve", lambda: nc.vector.tensor_reduce(
                        out=ss.t[:], in_=ysq.t[:].rearrange("p (h d) -> p h d", h=4), axis=AX.X, op=ALU.add),
                        [ss[:]], [ysq[:]])
                    P.act(ss[:], ss[:], AF.Sqrt, bias=EPS, scale=1.0 / 128)
                    P.recip(ss[:], ss[:])
                    P.tt("dve", gg[:], ogb[r][:], hgbc[:], ALU.mult)
                    y3 = yst.v(yst.t[:].rearrange("p (h d) -> p h d", h=4))
                    P.tt("dve", y3, y3, ss.v(ss.t[:].unsqueeze(2).to_broadcast([128, 4, 128])), ALU.mult)
                    P.tt("dve", yst[:], yst[:], gg[:], ALU.mult)
                    psT = psBB
                    for h in range(4):
                        P.tr(psT[:, h * 128:(h + 1) * 128], yst[:, h * 128:(h + 1) * 128], ident[:], partial=(h > 0))
                    P.copy("act", View(aT3[:, 0:4, t0:t0 + 128], aT_b[0:4]),
                           psT.v(psT.t[:].rearrange("p (h t) -> p h t", h=4)), partial=True)
            P.barrier()

    def even_layer(l):
        j = l // 2
        norm_phase(lambda kc, kind: cf(0, kc, kind), lambda kc, kind: cf(1, kc, kind), TB)
        even_inproj(j)
        conformer_ln(j)
        mlstm_pass(j, 0)
        mlstm_pass(j, 1)
        if dbg:
            dd = nc.dram_tensor("dbg_aT%d" % l, [128, KD * NT], BF16, kind="ExternalOutput").ap()
            dbg_names.append("dbg_aT%d" % l)
            P.dma("sp", View(dd, []), View(aT.t[:], aT_b))
            P.barrier()
        out_proj(ewout_d[j])

    class Sub:
        def __init__(self, tile, c0, n):
            self.t = tile.t
            self.c0 = c0
            self.n = n
            self.buf = tile.buf

        def v(self, a=0, b=None):
            b = self.n if b is None else b
            return View(self.t[:, self.c0 + a:self.c0 + b], [self.buf], True)

    if mixer_on and cfg.get("odd", True) and any(l_ % 2 == 1 for l_ in llist):
        owfm_d = din("o_w_fm", [2, 24, 128, KD, 128])
        owtm_d = din("o_w_tm", [2, 4, 128, KD, 256])
        owg_d = din("o_w_g", [2, 128, KD, 32])
        oshort_d = din("o_short", [2, 128, 24, 5])
        oalog_d = din("o_alog_bc", [2, 128, 16])
        odtb_d = din("o_dtb_bc", [2, 128, 16])
        ohg_d = din("o_head_g_bc", [2, 128, 128])
        owout_d = din("o_w_out", [2, KD, 128, KD, 128])

    def odd_inproj(j):
        SC = 128.0 ** -0.5
        with ExitStack() as ph:
            shw = ptile(ph, "shw", [128, 24 * 5])
            P.dma("pool", shw[:], View(oshort_d[j].rearrange("p b k -> p (b k)"), []))
            with ExitStack() as ph2:
                wbs = [ptile(ph2, "owfs%d" % i, [128, KD * 128]) for i in range(2)]
                wb = [ptile(ph2, "owf%d" % i, [128, KD * 128], BF16) for i in range(2)]
                st = [ptile(ph2, "ost%d" % i, [128, NT]) for i in range(1)]
                cvb = [ptile(ph2, "ocv%d" % i, [128, NT]) for i in range(2)]
                tks = ptile(ph2, "otk", [128, NTILE * 128])
                sqb = [ptile(ph2, "osq%d" % i, [128, 512]) for i in range(1)]
                rb = [ptile(ph2, "orr%d" % i, [128, 512]) for i in range(2)]
                npb = 0
                for blk in range(24):
                    w = wb[blk % 2]
                    load_w(w, owfm_d[j, blk].rearrange("p k f -> p (k f)"), wbs[blk % 2], KD * 128)
                    s = st[0]
                    cv = cvb[blk % 2]
                    for bi, (t0, tn) in enumerate(TB):
                        ps = PS[npb % 4]
                        npb += 1
                        for kc in range(KD):
                            P.mm(ps[:, 0:tn], w[:, kc * 128:(kc + 1) * 128], aTv(kc, t0, tn),
                                 start=(kc == 0), stop=(kc == KD - 1))
                        P.copy("act", s[:, t0:t0 + tn], ps[:, 0:tn], partial=True)

                    def wk(k):
                        return shw[:, blk * 5 + k:blk * 5 + k + 1]
                    P.ts("dve", cv[:], s[:], wk(2), None, ALU.mult)
                    for k in range(5):
                        if k == 2:
                            continue
                        off = k - 2
                        for (s0, s1) in ((0, NCTX), (NCTX, NT)):
                            a = max(s0, s0 - off)
                            b = min(s1, s1 - off)
                            P.stt("dve", cv[:, a:b], s[:, a + off:b + off], wk(k), cv[:, a:b], ALU.mult, ALU.add,
                                  partial=True)
                    P.act(cv[:], cv[:], AF.Silu)
                    if blk < 16:
                        for bi, (t0, tn) in enumerate(TB):
                            sq = sqb[0]
                            r = rb[bi % 2]
                            ps = PS[4 + bi % 2]
                            P.act(sq[:, 0:tn], cv[:, t0:t0 + tn], AF.Square)
                            P.mm(ps[:, 0:tn], ones128[:], sq[:, 0:tn], True, True)
                            P.act(r[:, 0:tn], ps[:, 0:tn], AF.Sqrt, bias=EPS, scale=1.0)
                            P.recip(r[:, 0:tn], r[:, 0:tn])
                            P.stt("dve", cv[:, t0:t0 + tn], cv[:, t0:t0 + tn], SC if blk < 8 else 1.0, r[:, 0:tn],
                                  ALU.mult, ALU.mult, partial=True)
                        dst = qT_s if blk < 8 else kT_s
                        hb = blk % 8
                        P.dma("pool", View(dst[hb * 128:(hb + 1) * 128, :], []), cv[:])
                    if blk >= 8:
                        hb = blk % 8
                        for q4 in range(0, NTILE, 4):
                            ps = PS[6 + (q4 // 4) % 2]
                            nq = min(4, NTILE - q4)
                            for q in range(nq):
                                ti = q4 + q
                                P.tr(ps[:, q * 128:(q + 1) * 128], cv[:, ti * 128:(ti + 1) * 128], ident[:],
                                     partial=(q > 0))
                            P.copy("act" if (q4 // 4) % 2 == 0 else "dve", tks[:, q4 * 128:(q4 + nq) * 128],
                                   ps[:, 0:nq * 128], partial=True)
                        dst = ktok_s if blk < 16 else vtok_s
                        P.dma("pool", View(dst.rearrange("(n p) f -> p n f", p=128)[:, :, hb * 128:(hb + 1) * 128], []),
                              tks.v(tks.t[:].rearrange("p (n f) -> p n f", f=128)))
                P.barrier()
            with ExitStack() as ph2:
                wts = [ptile(ph2, "owts%d" % i, [128, KD * 256]) for i in range(2)]
                wt = [ptile(ph2, "owt%d" % i, [128, KD * 256], BF16) for i in range(2)]
                wgs_ = ptile(ph2, "owgs", [128, KD * 32])
                wg_ = ptile(ph2, "owg", [128, KD * 32], BF16)
                alog = ptile(ph2, "oalog", [128, 16])
                dtb = ptile(ph2, "odtb", [128, 16])
                P.dma("pool", alog[:], View(oalog_d[j], []))
                P.dma("pool", dtb[:], View(odtb_d[j], []))
                P.act(alog[:], alog[:], AF.Exp)
                P.ts("dve", alog[:], alog[:], -1.0, None, ALU.mult)
                stg = [ptile(ph2, "otm%d" % i, [128, 256]) for i in range(3)]
                gst = [ptile(ph2, "ogs%d" % i, [128, 32]) for i in range(2)]
                gtmp = [ptile(ph2, "ogt%d" % i, [128, 8]) for i in range(2)]
                n = 0
                for g in range(4):
                    w = wt[g % 2]
                    load_w(w, owtm_d[j, g].rearrange("p k f -> p (k f)"), wts[g % 2], KD * 256)
                    for ti in range(NTILE):
                        ps = PS[n % 4]
                        s = stg[n % 3]
                        n += 1
                        for kc in range(KD):
                            P.mm(ps[:, 0:256], aTv(kc, ti * 128, 128), w[:, kc * 256:(kc + 1) * 256],
                                 start=(kc == 0), stop=(kc == KD - 1))
                        P.act(s[:], ps[:, 0:256], AF.Silu)
                        P.dma("pool", View(z_s[ti * 128:(ti + 1) * 128, g * 256:(g + 1) * 256], []), s[:])
                load_w(wg_, owg_d[j].rearrange("p k f -> p (k f)"), wgs_, KD * 32)
                for ti in range(NTILE):
                    ps = PS[4 + ti % 2]
                    s = gst[ti % 2]
                    tmp = gtmp[ti % 2]
                    for kc in range(KD):
                        P.mm(ps[:, 0:32], aTv(kc, ti * 128, 128), wg_[:, kc * 32:(kc + 1) * 32],
                             start=(kc == 0), stop=(kc == KD - 1))
                    for d_ in range(2):
                        c0 = d_ * 16
                        P.tt("dve", tmp[:], ps[:, c0:c0 + 8], dtb[:, d_ * 8:(d_ + 1) * 8], ALU.add)
                        P.act(tmp[:], tmp[:], AF.Exp)
                        P.act(tmp[:], tmp[:], AF.Ln, bias=1.0, scale=1.0)
                        P.tt("dve", s[:, c0:c0 + 8], tmp[:], alog[:, d_ * 8:(d_ + 1) * 8], ALU.mult, partial=True)
                        P.act(s[:, c0 + 8:c0 + 16], ps[:, c0 + 8:c0 + 16], AF.Sigmoid, partial=True)
                    P.dma("pool", View(gates_s[ti * 128:(ti + 1) * 128, 0:32], []), s[:])
                P.barrier()

    def gdn_pass(j, direction):
        fwd = direction == 0
        tri = triF if fwd else triB
        poss = possF if fwd else possB
        negi = negiF if fwd else negiB
        base = 0 if fwd else 16
        order = list(range(NTILE)) if fwd else [1, 0] + list(range(NTILE - 1, 1, -1))
        steps = [(c, hg) for c in order for hg in range(2)]
        steps = steps[:cfg.get("gdn_max_steps", len(steps))]
        with ExitStack() as ph:
            S2 = [[ptile(ph, "S2%d%d" % (h, g), [128, 256]) for g in range(2)] for h in range(2)]
            for h in range(2):
                for g in range(2):
                    P.memset("pool", S2[h][g][:], 0.0)
            qTb = [ptile(ph, "gq%d" % i, [128, 512]) for i in range(2)]
            kTb = [ptile(ph, "gk%d" % i, [128, 512]) for i in range(2)]
            ktb = [ptile(ph, "gkt%d" % i, [128, 512]) for i in range(2)]
            vtb = [ptile(ph, "gvt%d" % i, [128, 512]) for i in range(2)]
            gtb = [ptile(ph, "gg%d" % i, [128, 32]) for i in range(2)]
            if not fwd:
                yfb = [ptile(ph, "gyf%d" % i, [128, 512]) for i in range(2)]
                zsb = [ptile(ph, "gzs%d" % i, [128, 512]) for i in range(2)]
                hgbc = ptile(ph, "ghg", [128, 128])
                P.dma("pool", hgbc[:], View(ohg_d[j], []))
                ssb = [ptile(ph, "gss%d" % i, [128, 4]) for i in range(2)]
            G2b = [ptile(ph, "gG2%d" % i, [128, 512]) for i in range(2)]
            egcBb = [ptile(ph, "gegB%d" % i, [128, 512]) for i in range(2)]
            cnames = ["gcol", "ngcol", "egc", "bg", "bE", "kdw", "eGl"]
            cols = [{nm: ptile(ph, "c%s%d" % (nm, i), [128, 4]) for nm in cnames} for i in range(2)]
            tnames = ["E", "ET", "MT", "P0", "P1", "PT0", "PT1", "XT0", "XT1", "kdec"]
            TT = [{nm: ptile(ph, "t2%s%d" % (nm, g), [128, 256]) for nm in tnames} for g in range(2)]
            ystb = [ptile(ph, "gyst%d" % i, [128, 512]) for i in range(2)]
            psG, psCol = PS[0], PS[1]
            BK = [(PS[2], PS[3], PS[4]), (PS[5], PS[6], PS[7])]
            psT = PS[2]

            def v2(t_, c0=0):
                return t_.v(t_.t[:, c0:c0 + 256].rearrange("p (h s) -> p h s", h=2))

            def cbc2(ct, gi):
                return ct.v(ct.t[:, gi * 2:gi * 2 + 2].unsqueeze(2).to_broadcast([128, 2, 128]))

            def mbc2(m):
                return m.v(m.t[:].unsqueeze(1).to_broadcast([128, 2, 128]))

            def load(si):
                r = si % 2
                c, hg = steps[si]
                t0 = c * 128
                f0 = hg * 512
                P.dma("sp", qTb[r].v(qTb[r].t[:].rearrange("p (h t) -> p h t", h=4)),
                      View(qT_s[f0:f0 + 512, t0:t0 + 128].rearrange("(h p) t -> p h t", p=128), []))
                P.dma("sp", kTb[r].v(kTb[r].t[:].rearrange("p (h t) -> p h t", h=4)),
                      View(kT_s[f0:f0 + 512, t0:t0 + 128].rearrange("(h p) t -> p h t", p=128), []))
                P.dma("sp", ktb[r][:], View(ktok_s[t0:t0 + 128, f0:f0 + 512], []))
                P.dma("sp", vtb[r][:], View(vtok_s[t0:t0 + 128, f0:f0 + 512], []))
                P.dma("sp", gtb[r][:], View(gates_s[t0:t0 + 128, 0:32], []))
                if not fwd:
                    P.dma("sp", yfb[r][:], View(yf_s[t0:t0 + 128, f0:f0 + 512], []))
                    P.dma("sp", zsb[r][:], View(z_s[t0:t0 + 128, f0:f0 + 512], []))

            def prep(si):
                r = si % 2
                c, hg = steps[si]
                gt = gtb[r]
                G2, egcB, cl = G2b[r], egcBb[r], cols[r]
                g0 = base + hg * 4
                gv = gt[:, g0:g0 + 4]
                beta = gt[:, g0 + 8:g0 + 12]
                pc = r * 8
                P.tt("dve", G2.v(G2.t[:].rearrange("p (h s) -> p h s", h=4)),
                     tri.v(tri.t[:].unsqueeze(1).to_broadcast([128, 4, 128])),
                     gt.v(gt.t[:, g0:g0 + 4].unsqueeze(2).to_broadcast([128, 4, 128])), ALU.mult)
                P.mm(psG[:, 0:512], ones128[:], G2[:], True, True)
                P.mm_part(psCol[:, pc:pc + 4], tri[:], gv, True, True, partial=False)
                P.mm_part(psCol[:, pc + 4:pc + 8], ones128[:], gv, True, True, partial=True)
                P.copy("act", cl["gcol"][:], psCol[:, pc:pc + 4])
                P.ts("dve", cl["ngcol"][:], psCol[:, pc:pc + 4], -1.0, None, ALU.mult)
                P.act(cl["egc"][:], psCol[:, pc:pc + 4], AF.Exp)
                P.tt("dve", cl["bg"][:], beta, cl["egc"][:], ALU.mult)
                P.act(cl["bE"][:], beta, AF.Ln)
                P.tt("dve", cl["bE"][:], cl["bE"][:], cl["gcol"][:], ALU.add)
                P.tt("dve", cl["kdw"][:], psCol[:, pc + 4:pc + 8], cl["gcol"][:], ALU.subtract)
                P.act(cl["kdw"][:], cl["kdw"][:], AF.Exp)
                P.act(cl["eGl"][:], psCol[:, pc + 4:pc + 8], AF.Exp)
                P.act(egcB[:], psG[:, 0:512], AF.Exp)

            load(0)
            if len(steps) > 1:
                load(1)
            prep(0)
            GR = range(2)
            for si, (c, hg) in enumerate(steps):
                r = si % 2
                t0 = c * 128
                qTr, kTr, ktr, vtr, gt = qTb[r], kTb[r], ktb[r], vtb[r], gtb[r]
                G2, egcB, cl = G2b[r], egcBb[r], cols[r]
                g0 = base + hg * 4
                yst = ystb[r]
                col = lambda nm, h_: cl[nm][:, h_:h_ + 1]
                HS = lambda hh: slice(hh * 128, (hh + 1) * 128)
                LS = lambda li: slice(li * 128, (li + 1) * 128)
                heads = lambda gi: (2 * gi, 2 * gi + 1)
                for gi in GR:
                    bA, bB, bC = BK[gi]
                    for hh in heads(gi):
                        P.mm(bA[:, LS(hh % 2)], kTr[:, HS(hh)], kTr[:, HS(hh)], True, True)
                        P.mm(bB[:, LS(hh % 2)], kTr[:, HS(hh)], qTr[:, HS(hh)], True, True)
                for gi in GR:
                    T = TT[gi]
                    pg = psG.v(psG.t[:, gi * 256:(gi + 1) * 256].rearrange("p (h s) -> p h s", h=2))
                    P.stt("dve", v2(T["E"]), pg, -1.0, mbc2(poss), ALU.mult, ALU.subtract)
                    P.tt("dve", v2(T["ET"]), pg, mbc2(negi), ALU.add)
                for gi in GR:
                    T = TT[gi]
                    for hh in heads(gi):
                        li = hh % 2
                        P.act(T["E"][:, LS(li)], T["E"][:, LS(li)], AF.Exp, bias=col("bE", hh), scale=1.0, partial=True)
                        P.act(T["ET"][:, LS(li)], T["ET"][:, LS(li)], AF.Exp, bias=col("ngcol", hh), scale=1.0,
                              partial=True)
                for gi in GR:
                    T = TT[gi]
                    bA, bB, bC = BK[gi]
                    P.tt("dve", T["P0"][:], bA[:, 0:256], T["E"][:], ALU.mult)
                    P.tt("dve", T["MT"][:], bB[:, 0:256], T["ET"][:], ALU.mult)
                for gi in GR:
                    T = TT[gi]
                    bA, bB, bC = BK[gi]
                    for li in range(2):
                        P.tr(bC[:, LS(li)], T["P0"][:, LS(li)], ident[:], partial=(li > 0))
                for gi in GR:
                    T = TT[gi]
                    bA, bB, bC = BK[gi]
                    P.copy("act", T["PT0"][:], bC[:, 0:256])
                    P.stt("dve", v2(T["XT0"]), bC.v(bC.t[:, 0:256].rearrange("p (h s) -> p h s", h=2)), -1.0,
                          mbc2(ident), ALU.mult, ALU.add)
                for k in range(1, 8):
                    pv, cu = (k - 1) % 2, k % 2
                    for gi in GR:
                        T = TT[gi]
                        bA, bB, bC = BK[gi]
                        Pp, PTp = T["P%d" % pv], T["PT%d" % pv]
                        for li in range(2):
                            if k <= 6:
                                P.mm(bA[:, LS(li)], PTp[:, LS(li)], Pp[:, LS(li)], True, True)
                            if k < 6:
                                P.mm(bB[:, LS(li)], Pp[:, LS(li)], PTp[:, LS(li)], True, True)
                            if k >= 2:
                                P.mm(bC[:, LS(li)], Pp[:, LS(li)], T["XT%d" % cu][:, LS(li)], True, True)
                    for gi in GR:
                        T = TT[gi]
                        bA, bB, bC = BK[gi]
                        if k <= 6:
                            P.copy("act", T["P%d" % cu][:], bA[:, 0:256])
                        if k >= 2:
                            P.tt("dve", T["XT%d" % pv][:], bC[:, 0:256], T["XT%d" % cu][:], ALU.add)
                        if k < 6:
                            P.copy("act" if (k + gi) % 2 == 0 else "dve", T["PT%d" % cu][:], bB[:, 0:256])
                    if k == 3 and si + 1 < len(steps):
                        prep(si + 1)
                for gi in GR:
                    T = TT[gi]
                    c0 = gi * 256
                    P.tt("dve", v2(T["E"]), v2(vtr, c0),
                         gt.v(gt.t[:, g0 + 8 + 2 * gi:g0 + 10 + 2 * gi].unsqueeze(2).to_broadcast([128, 2, 128])),
                         ALU.mult)
                    P.tt("dve", v2(T["ET"]), v2(ktr, c0), cbc2(cl["bg"], gi), ALU.mult)
                    P.tt("dve", v2(T["kdec"]), v2(ktr, c0), cbc2(cl["kdw"], gi), ALU.mult)
                    P.tt("dve", T["PT1"][:], qTr[:, c0:c0 + 256], egcB[:, c0:c0 + 256], ALU.mult)
                for gi in GR:
                    T = TT[gi]
                    bA, bB, bC = BK[gi]
                    for li in range(2):
                        P.mm(bA[:, LS(li)], T["XT0"][:, LS(li)], T["E"][:, LS(li)], True, True)
                        P.mm(bB[:, LS(li)], T["ET"][:, LS(li)], T["XT0"][:, LS(li)], True, True)
                for gi in GR:
                    T = TT[gi]
                    bA, bB, bC = BK[gi]
                    P.copy("act", T["P0"][:], bB[:, 0:256])
                    P.copy("act", T["P1"][:], bA[:, 0:256])
                for gi in GR:
                    T = TT[gi]
                    bA, bB, bC = BK[gi]
                    S = S2[hg][gi]
                    for li in range(2):
                        P.mm(bC[:, LS(li)], T["P0"][:, LS(li)], S[:, LS(li)], True, True)
                for gi in GR:
                    T = TT[gi]
                    bA, bB, bC = BK[gi]
                    P.tt("dve", T["PT0"][:], T["P1"][:], bC[:, 0:256], ALU.subtract)
                for gi in GR:
                    T = TT[gi]
                    bA, bB, bC = BK[gi]
                    S = S2[hg][gi]
                    for li in range(2):
                        P.mm(bB[:, LS(li)], T["PT1"][:, LS(li)], S[:, LS(li)], True, False)
                        P.mm(bB[:, LS(li)], T["MT"][:, LS(li)], T["PT0"][:, LS(li)], False, True)
                    for li in range(2):
                        P.mm(bA[:, LS(li)], T["kdec"][:, LS(li)], T["PT0"][:, LS(li)], True, True)
                for gi in GR:
                    T = TT[gi]
                    bA, bB, bC = BK[gi]
                    S = S2[hg][gi]
                    c0 = gi * 256
                    if fwd:
                        P.copy("act", yst[:, c0:c0 + 256], bB[:, 0:256], partial=True)
                    else:
                        P.tt("dve", yst[:, c0:c0 + 256], bB[:, 0:256], yfb[r][:, c0:c0 + 256], ALU.add, partial=True)
                    P.tt("dve", v2(S), v2(S), cbc2(cl["eGl"], gi), ALU.mult)
                    P.tt("dve", S[:], S[:], bA[:, 0:256], ALU.add)
                if fwd:
                    P.dma("pool", View(yf_s[t0:t0 + 128, hg * 512:(hg + 1) * 512], []), yst[:])
                else:
                    ss = ssb[r]
                    ysq = G2
                    P.tt("dve", ysq[:], yst[:], yst[:], ALU.mult)
                    P.op("dve", lambda: nc.vector.tensor_reduce(
                        out=ss.t[:], in_=ysq.t[:].rearrange("p (h d) -> p h d", h=4), axis=AX.X, op=ALU.add),
                        [ss[:]], [ysq[:]])
                    P.act(ss[:], ss[:], AF.Sqrt, bias=EPS, scale=1.0 / 128)
                    P.recip(ss[:], ss[:])
                    y3 = yst.v(yst.t[:].rearrange("p (h d) -> p h d", h=4))
                    z3 = zsb[r].v(zsb[r].t[:].rearrange("p (h d) -> p h d", h=4))
                    P.tt("dve", z3, z3, hgbc.v(hgbc.t[:].unsqueeze(1).to_broadcast([128, 4, 128])), ALU.mult)
                    P.tt("dve", y3, y3, ss.v(ss.t[:].unsqueeze(2).to_broadcast([128, 4, 128])), ALU.mult)
                    P.tt("dve", yst[:], yst[:], zsb[r][:], ALU.mult)
                    for hh in range(4):
                        P.tr(psT[:, hh * 128:(hh + 1) * 128], yst[:, hh * 128:(hh + 1) * 128], ident[:],
                             partial=(hh > 0))
                    P.copy("act", View(aT3[:, hg * 4:(hg + 1) * 4, t0:t0 + 128], aT_b[hg * 4:(hg + 1) * 4]),
                           psT.v(psT.t[:].rearrange("p (h t) -> p h t", h=4)), partial=True)
                if si + 2 < len(steps):
                    load(si + 2)
            P.barrier()

    def odd_layer(l):
        j = l // 2
        norm_phase(lambda kc, kind: cf(0, kc, kind), lambda kc, kind: cf(1, kc, kind), TB)
        odd_inproj(j)
        if not cfg.get("skip_gdn", False):
            gdn_pass(j, 0)
            if not cfg.get("skip_gdn_b", False):
                gdn_pass(j, 1)
        if dbg:
            dd = nc.dram_tensor("dbg_aT%d" % l, [128, KD * NT], BF16, kind="ExternalOutput").ap()
            dbg_names.append("dbg_aT%d" % l)
            P.dma("sp", View(dd, []), View(aT.t[:], aT_b))
            P.barrier()
        out_proj(owout_d[j])

    for l in cfg.get("layer_list", list(range(nlayers))):
        modulation(l)
        if mixer_on:
            if l % 2 == 0:
                even_layer(l)
            elif cfg.get("odd", True):
                odd_layer(l)
        if ffn_on:
            norm_phase(lambda kc, kind: cf(3, kc, kind), lambda kc, kind: cf(4, kc, kind), TB)
            ffn_phase(l)

    fing = sb("fing", [128, KD])
    P.dma("sp", fing[:], View(fing_d, []))
    with ExitStack() as ph:
        sq = [ptile(ph, "fsq%d" % i, [128, 512]) for i in range(3)]
        rs = [ptile(ph, "frs%d" % i, [128, 512]) for i in range(2)]
        fo = [ptile(ph, "ffo%d" % i, [128, KD * 512]) for i in range(2)]
        stg = [ptile(ph, "sto%d" % i, [128, D]) for i in range(2)]
        n = 0
        nt_ = 0
        for bi, (t0, tn) in enumerate(TB[1:]):
            ps = PS[4 + bi % 2]
            for kc in range(KD):
                s = sq[n % 3]
                n += 1
                P.act(s[:, 0:tn], xTv(kc, t0, tn), AF.Square)
                P.mm(ps[:, 0:tn], onesD[:], s[:, 0:tn], start=(kc == 0), stop=(kc == KD - 1))
            r = rs[bi % 2]
            P.act(r[:, 0:tn], ps[:, 0:tn], AF.Sqrt, bias=EPS, scale=1.0)
            P.recip(r[:, 0:tn], r[:, 0:tn])
            f = fo[bi % 2]
            for kc in range(KD):
                P.stt("dve", f[:, kc * 512:kc * 512 + tn], xTv(kc, t0, tn), fing[:, kc:kc + 1], r[:, 0:tn],
                      ALU.mult, ALU.mult, partial=True)
            for q in range(tn // 128):
                ti = (t0 - NCTX) // 128 + q
                s_ = stg[nt_ % 2]
                for half in range(2):
                    pst = PS[(nt_ * 2 + half) % 4]
                    for qq in range(4):
                        kc = half * 4 + qq
                        P.tr(pst[:, qq * 128:(qq + 1) * 128], f[:, kc * 512 + q * 128:kc * 512 + (q + 1) * 128],
                             ident[:], partial=(qq > 0))
                    P.copy("act" if half == 0 else "dve", s_[:, half * 512:(half + 1) * 512], pst[:], partial=True)
                nt_ += 1
                P.dma("sp", View(out_d[ti * 128:(ti + 1) * 128, :], []), s_[:])
        P.barrier()
    es.close()
    return nc, P, dbg_names


def _blk_w(w, nblk):
    K = w.shape[0]
    return np.ascontiguousarray(w.reshape(K // 128, 128, nblk, 128).transpose(2, 1, 0, 3))


def _pm(v):
    return np.ascontiguousarray(v.reshape(-1, 128).T)


def prepare_inputs(inp):
    f = lambda a: np.asarray(a, dtype=np.float32)
    shared = {}
    shared["ada_w"] = np.stack([_blk_w(f(inp["ada_w"][l]), 48) for l in range(DEPTH)])
    shared["ada_b"] = np.stack([_pm(f(inp["ada_b"][l])) for l in range(DEPTH)])
    shared["norm1_g"] = np.stack([_pm(f(inp["norm1_g"][l])) for l in range(DEPTH)])
    shared["norm2_g"] = np.stack([_pm(f(inp["norm2_g"][l])) for l in range(DEPTH)])
    shared["final_g"] = _pm(f(inp["final_g"]))
    shared["f_w_up"] = np.stack([_blk_w(f(inp["f_w_up"][l]), 2 * NJ) for l in range(DEPTH)])
    fd = f(inp["f_dw"]).reshape(DEPTH, 9, NJ, 128).transpose(0, 3, 2, 1)
    shared["f_dw"] = np.ascontiguousarray(fd)
    shared["f_dw_b"] = np.stack([_pm(f(inp["f_dw_b"][l])) for l in range(DEPTH)])
    wd = f(inp["f_w_down"]).reshape(DEPTH, NJ, 128, KD, 128).transpose(0, 3, 2, 1, 4)
    shared["f_w_down"] = np.ascontiguousarray(wd)
    def wb(wc):
        return np.ascontiguousarray(wc.reshape(KD, 128, wc.shape[1]).transpose(1, 0, 2))

    def bc(v):
        return np.ascontiguousarray(np.broadcast_to(v[None, :], (128, v.shape[0])))

    ew = f(inp["e_w_in"])
    eb = f(inp["e_b_in"])
    fm_cols = [h * 128 for h in range(4)] + [512 + h * 128 for h in range(4)]
    for i in range(4):
        fm_cols += [2064 + i * 128, 2576 + i * 128]
    shared["e_w_fm"] = np.stack([np.stack([wb(ew[j][:, c:c + 128]) for c in fm_cols]) for j in range(2)])
    shared["e_b_fm"] = np.stack([np.stack([eb[j][c:c + 128] for c in fm_cols], axis=1) for j in range(2)])
    tm_cols = [512, 768, 1024, 1280, 1536, 1792]
    shared["e_w_tm"] = np.stack([np.stack([wb(ew[j][:, c:c + 256]) for c in tm_cols]) for j in range(2)])
    shared["e_w_g"] = np.stack([wb(ew[j][:, 2048:2064]) for j in range(2)])
    shared["e_b_tm"] = np.stack([bc(eb[j][512:2064]) for j in range(2)])
    shared["e_head_g_bc"] = np.stack([bc(f(inp["e_head_g"])[j]) for j in range(2)])
    shared["e_cdw"] = np.ascontiguousarray(f(inp["e_conf_dw"]).reshape(2, 31, 4, 128).transpose(0, 3, 2, 1))
    shared["e_cdwb"] = np.stack([_pm(f(inp["e_conf_dw_b"])[j]) for j in range(2)])
    shared["e_lng"] = np.stack([_pm(f(inp["e_conf_ln_g"])[j]) for j in range(2)])
    shared["e_lnb"] = np.stack([_pm(f(inp["e_conf_ln_b"])[j]) for j in range(2)])
    shared["e_w_out"] = np.stack([_blk_w(f(inp["e_w_out"])[j], KD) for j in range(2)])
    ow = f(inp["o_w_in"])
    shared["o_w_fm"] = np.stack([np.stack([wb(ow[j][:, b * 128:(b + 1) * 128]) for b in range(24)]) for j in range(2)])
    shared["o_w_tm"] = np.stack([np.stack([wb(ow[j][:, 3072 + g * 256:3072 + (g + 1) * 256]) for g in range(4)]) for j in range(2)])
    shared["o_w_g"] = np.stack([wb(ow[j][:, 4096:4128]) for j in range(2)])
    shared["o_short"] = np.ascontiguousarray(f(inp["o_short_w"]).reshape(2, 5, 24, 128).transpose(0, 3, 2, 1))
    shared["o_alog_bc"] = np.stack([bc(f(inp["o_a_log"])[j].reshape(16)) for j in range(2)])
    shared["o_dtb_bc"] = np.stack([bc(f(inp["o_dt_bias"])[j].reshape(16)) for j in range(2)])
    shared["o_head_g_bc"] = np.stack([bc(f(inp["o_head_g"])[j]) for j in range(2)])
    shared["o_w_out"] = np.stack([_blk_w(f(inp["o_w_out"])[j], KD) for j in range(2)])
    per_core = []
    x = f(inp["x"])
    ctx = f(inp["ctx"])
    c = f(inp["c"])
    cc = f(inp["c_ctx"])
    for b in range(8):
        d = dict(shared)
        d["xc"] = np.ascontiguousarray(np.concatenate([ctx[b], x[b]], axis=0))
        cv = np.stack([c[b], cc], axis=1)
        d["cvec"] = np.ascontiguousarray(cv.reshape(KD, 128, 2).transpose(1, 0, 2))
        per_core.append(d)
    return per_core


_CFG = {}


def kernel(**inputs):
    nc, P, _ = build_program(_CFG)
    in_maps = prepare_inputs(inputs)
    if not _CFG.get("ffn", True):
        in_maps = [{k: v for k, v in d.items() if not k.startswith("f_")} for d in in_maps]
    if not _CFG.get("mixer", True):
        in_maps = [{k: v for k, v in d.items() if not (k.startswith("e_") or k.startswith("o_"))} for d in in_maps]
    elif not _CFG.get("odd", True) or _CFG.get("layers", DEPTH) < 2:
        in_maps = [{k: v for k, v in d.items() if not k.startswith("o_")} for d in in_maps]
    res = run_bass_kernel_spmd(nc, in_maps, core_ids=list(range(8)))
    return np.stack([np.asarray(r["out"], dtype=np.float32) for r in res.results], axis=0)
```
